# Optimizing a Trainium2 kernel written in Bass

```python
import jax, jax.numpy as jnp
from jax import lax
import numpy as np

D_MODEL = 2048
BATCH = 8
SEQ = 2048
DEPTH = 2

GRID_W = 64
CTX_LEN = 256
EPS = 1e-6
NEG_INF = -1e30
HEAD_DIM = 128
A_Q_HEADS = (D_MODEL // 2) // HEAD_DIM
A_KV_HEADS = 2
A_GROUP = A_Q_HEADS // A_KV_HEADS
WINDOW = 128
BLOCK_Q = 128
ROPE_BASE = 10000.0
B_HEADS = 4
B_DV = (D_MODEL // 2) // B_HEADS
B_DK = B_DV // 2
B_GATE_RANK = 16
B_GATE_NORM = 16.0
B_CHUNK = 64
C_GROUPS = 8
C_GROUP_DIM = D_MODEL // C_GROUPS
D_FF = 4 * D_MODEL

N_EVEN = (DEPTH + 1) // 2
N_ODD = DEPTH // 2
A_Q_W = A_Q_HEADS * HEAD_DIM
A_KV_W = A_KV_HEADS * HEAD_DIM
B_QK_W = B_HEADS * B_DK
B_V_W = B_HEADS * B_DV
IN_SPLITS = (A_Q_W, A_KV_W, A_KV_W, B_QK_W, B_QK_W, B_V_W, B_V_W, B_GATE_RANK, B_GATE_RANK)
IN_COLS = sum(IN_SPLITS)
MIX_OUT = A_Q_W + B_V_W

kernel_name = "hybrid_dit_window_gla_fourier"


def rms_norm(x, g):
    xf = x.astype(jnp.float32)
    y = xf * lax.rsqrt(jnp.mean(xf * xf, axis=-1, keepdims=True) + EPS)
    return (y * g.astype(jnp.float32)).astype(x.dtype)


def modulate(h, shift, scale):
    return h * (1 + scale) + shift


def heads(t, n):
    b, l, _ = t.shape
    return t.reshape(b, l, n, -1).transpose(0, 2, 1, 3)


def merge_heads(t):
    b, n, l, d = t.shape
    return t.transpose(0, 2, 1, 3).reshape(b, l, n * d)


def flip(t):
    return t[:, :, ::-1]


def axial_rope_angles(n_rows):
    row = jnp.repeat(jnp.arange(n_rows), GRID_W).astype(jnp.float32)
    col = jnp.tile(jnp.arange(GRID_W), n_rows).astype(jnp.float32)
    n_freq = HEAD_DIM // 4
    inv_freq = ROPE_BASE ** (-jnp.arange(n_freq, dtype=jnp.float32) / n_freq)
    return row[:, None] * inv_freq, col[:, None] * inv_freq


def rope_1d(x, ang):
    n = ang.shape[-1]
    cos, sin = jnp.cos(ang).astype(x.dtype), jnp.sin(ang).astype(x.dtype)
    x1, x2 = x[..., :n], x[..., n:]
    return jnp.concatenate([x1 * cos - x2 * sin, x2 * cos + x1 * sin], axis=-1)


def apply_axial_rope(x, ang_r, ang_c):
    half = HEAD_DIM // 2
    return jnp.concatenate([rope_1d(x[..., :half], ang_r), rope_1d(x[..., half:], ang_c)], axis=-1)


def ab_project(h, w_in, q_norm, k_norm, gk_f, gk_f_b, gk_b, gk_b_b):
    parts = jnp.split(h @ w_in, np.cumsum(IN_SPLITS)[:-1].tolist(), axis=-1)
    aq, ak, av, bq, bk, bv, bgate, lr_f, lr_b = parts
    aq = rms_norm(heads(aq, A_Q_HEADS), q_norm)
    ak = rms_norm(heads(ak, A_KV_HEADS), k_norm)
    av = heads(av, A_KV_HEADS)

    def log_decay(lr, w, b):
        return jax.nn.log_sigmoid((lr @ w + b).astype(jnp.float32)) / B_GATE_NORM

    gf = heads(log_decay(lr_f, gk_f, gk_f_b), B_HEADS)
    gb = heads(log_decay(lr_b, gk_b, gk_b_b), B_HEADS)
    bq = heads(bq.astype(jnp.float32), B_HEADS) * B_DK ** -0.5
    bk = heads(bk.astype(jnp.float32), B_HEADS)
    bv = heads(bv.astype(jnp.float32), B_HEADS)
    return aq, ak, av, bq, bk, bv, gf, gb, bgate


def window_attention(q, k, v, kc, vc, sink):
    bsz, _, n_lat, d = q.shape
    nb = n_lat // BLOCK_Q
    qb = q.reshape(bsz, A_KV_HEADS, A_GROUP, nb, BLOCK_Q, d)
    pad = ((0, 0), (0, 0), (BLOCK_Q, BLOCK_Q), (0, 0))
    kp = jnp.pad(k, pad).reshape(bsz, A_KV_HEADS, nb + 2, BLOCK_Q, d)
    vp = jnp.pad(v, pad).reshape(bsz, A_KV_HEADS, nb + 2, BLOCK_Q, d)

    def band(t):
        return jnp.concatenate([t[:, :, :-2], t[:, :, 1:-1], t[:, :, 2:]], axis=3)

    kw, vw = band(kp), band(vp)
    scale = d ** -0.5
    s_lat = jnp.einsum("bkgnqd,bknsd->bkgnqs", qb, kw, preferred_element_type=jnp.float32) * scale
    s_ctx = jnp.einsum("bkgnqd,bkcd->bkgnqc", qb, kc, preferred_element_type=jnp.float32) * scale
    blk = jnp.arange(nb)[:, None, None]
    qpos = blk * BLOCK_Q + jnp.arange(BLOCK_Q)[None, :, None]
    kpos = (blk - 1) * BLOCK_Q + jnp.arange(3 * BLOCK_Q)[None, None, :]
    valid = (jnp.abs(qpos - kpos) <= WINDOW) & (kpos >= 0) & (kpos < n_lat)
    s_lat = jnp.where(valid, s_lat, NEG_INF)
    s_sink = jnp.broadcast_to(sink.astype(jnp.float32).reshape(A_KV_HEADS, A_GROUP, 1, 1, 1),
                              s_lat.shape[:-1] + (1,))
    p = jax.nn.softmax(jnp.concatenate([s_lat, s_ctx, s_sink], axis=-1), axis=-1).astype(v.dtype)
    n_w = 3 * BLOCK_Q
    n_ctx = kc.shape[2]
    o = (jnp.einsum("bkgnqs,bknsd->bkgnqd", p[..., :n_w], vw)
         + jnp.einsum("bkgnqc,bkcd->bkgnqd", p[..., n_w:n_w + n_ctx], vc))
    return o.reshape(bsz, A_Q_HEADS, n_lat, d)


def context_attention(qc, kc, vc, sink):
    bsz, _, n_ctx, d = qc.shape
    qg = qc.reshape(bsz, A_KV_HEADS, A_GROUP, n_ctx, d)
    s = jnp.einsum("bkgqd,bkcd->bkgqc", qg, kc, preferred_element_type=jnp.float32) * d ** -0.5
    s_sink = jnp.broadcast_to(sink.astype(jnp.float32).reshape(A_KV_HEADS, A_GROUP, 1, 1), s.shape[:-1] + (1,))
    p = jax.nn.softmax(jnp.concatenate([s, s_sink], axis=-1), axis=-1)[..., :n_ctx].astype(vc.dtype)
    o = jnp.einsum("bkgqc,bkcd->bkgqd", p, vc)
    return o.reshape(bsz, A_Q_HEADS, n_ctx, d)


def gla_chunked(q, k, v, g, s0):
    bsz, nh, n_tok, dk = q.shape
    dv = v.shape[-1]
    n = n_tok // B_CHUNK

    def rs(t):
        return t.reshape(bsz, nh, n, B_CHUNK, t.shape[-1])

    q, k, v, g = rs(q), rs(k), rs(v), rs(g)
    b = jnp.cumsum(g, axis=3)
    b_last = b[:, :, :, -1:]
    qe = q * jnp.exp(b)
    ke = k * jnp.exp(-b)
    kd = k * jnp.exp(b_last - b)
    causal = jnp.tril(jnp.ones((B_CHUNK, B_CHUNK), dtype=bool))
    a = jnp.where(causal, jnp.einsum("bhncd,bhnsd->bhncs", qe, ke), 0.0)
    o_intra = jnp.einsum("bhncs,bhnse->bhnce", a, v)

    def step(state, xs):
        qe_c, kd_c, v_c, dec = xs
        o = jnp.einsum("bhcd,bhde->bhce", qe_c, state)
        state = dec[..., None] * state + jnp.einsum("bhcd,bhce->bhde", kd_c, v_c)
        return state, o

    xs = (jnp.moveaxis(qe, 2, 0), jnp.moveaxis(kd, 2, 0), jnp.moveaxis(v, 2, 0),
          jnp.moveaxis(jnp.exp(b[:, :, :, -1]), 2, 0))
    s_fin, o_inter = lax.scan(step, s0, xs)
    o = o_intra + jnp.moveaxis(o_inter, 0, 2)
    return o.reshape(bsz, nh, n_tok, dv), s_fin


def gla_final_state(k, v, g):
    b = jnp.cumsum(g, axis=2)
    return jnp.einsum("bhtd,bhte->bhde", k * jnp.exp(b[:, :, -1:] - b), v)


def ab_output(o_a, o_b, gate, gla_norm, w_out):
    o_b = merge_heads(rms_norm(o_b, gla_norm)).astype(gate.dtype) * jax.nn.silu(gate)
    return jnp.concatenate([merge_heads(o_a), o_b], axis=-1) @ w_out


def ab_mixer(h, hc, need_ctx_out, w_in, q_norm, k_norm, sink, gk_f, gk_f_b, gk_b, gk_b_b,
             gla_norm, w_out, ang_r, ang_c):
    aq, ak, av, bq, bk, bv, gf, gb, bgate = ab_project(h, w_in, q_norm, k_norm, gk_f, gk_f_b, gk_b, gk_b_b)
    cq, ck, cv, cbq, cbk, cbv, cgf, cgb, cgate = ab_project(hc, w_in, q_norm, k_norm, gk_f, gk_f_b, gk_b, gk_b_b)
    aq = apply_axial_rope(aq, ang_r, ang_c)
    ak = apply_axial_rope(ak, ang_r, ang_c)
    o_a = window_attention(aq, ak, av, ck, cv, sink)
    yc = None
    if need_ctx_out:
        zero = jnp.zeros((hc.shape[0], B_HEADS, B_DK, B_DV), jnp.float32)
        oc_f, s_f = gla_chunked(cbq, cbk, cbv, cgf, zero)
        oc_b, s_b = gla_chunked(flip(cbq), flip(cbk), flip(cbv), flip(cgb), zero)
        oc_a = context_attention(cq, ck, cv, sink)
        yc = ab_output(oc_a, oc_f + flip(oc_b), cgate, gla_norm, w_out)
    else:
        s_f = gla_final_state(cbk, cbv, cgf)
        s_b = gla_final_state(flip(cbk), flip(cbv), flip(cgb))
    o_f, _ = gla_chunked(bq, bk, bv, gf, s_f)
    o_b, _ = gla_chunked(flip(bq), flip(bk), flip(bv), flip(gb), s_b)
    y = ab_output(o_a, o_f + flip(o_b), bgate, gla_norm, w_out)
    return y, yc


def fourier_mixer(h, w_out, b_out):
    bsz, n_tok, _ = h.shape
    hg = h.astype(jnp.float32).reshape(bsz, n_tok, C_GROUPS, C_GROUP_DIM)
    f = jnp.fft.fft2(hg, axes=(1, 3), norm="ortho").real.reshape(bsz, n_tok, D_MODEL)
    return f.astype(h.dtype) @ w_out + b_out


def sq_relu_mlp(h, w1, w2):
    return jnp.square(jax.nn.relu(h @ w1)) @ w2


def setup_inputs(seed: int = 0) -> dict:
    key = jax.random.key(seed)
    ks = jax.random.split(key, 22)

    def nrm(k, shape, scale):
        return jax.random.normal(k, shape, jnp.float32) * scale

    def gain(k, shape):
        return 1.0 + 0.02 * jax.random.normal(k, shape, jnp.float32)

    return {
        "x": nrm(ks[0], (BATCH, SEQ, D_MODEL), 1.0),
        "c": nrm(ks[1], (BATCH, D_MODEL), 1.0),
        "ctx": nrm(ks[2], (BATCH, CTX_LEN, D_MODEL), 1.0),
        "c_ctx": nrm(ks[3], (D_MODEL,), 1.0),
        "ada_w": nrm(ks[4], (DEPTH, D_MODEL, 6 * D_MODEL), D_MODEL ** -0.5),
        "ada_b": nrm(ks[5], (DEPTH, 6 * D_MODEL), 0.02),
        "norm_mix": gain(ks[6], (DEPTH, D_MODEL)),
        "norm_mlp": gain(ks[7], (DEPTH, D_MODEL)),
        "mlp_w1": nrm(ks[8], (DEPTH, D_MODEL, D_FF), D_MODEL ** -0.5),
        "mlp_w2": nrm(ks[9], (DEPTH, D_FF, D_MODEL), D_FF ** -0.5),
        "ab_w_in": nrm(ks[10], (N_EVEN, D_MODEL, IN_COLS), D_MODEL ** -0.5),
        "ab_q_norm": gain(ks[11], (N_EVEN, HEAD_DIM)),
        "ab_k_norm": gain(ks[12], (N_EVEN, HEAD_DIM)),
        "ab_sink": nrm(ks[13], (N_EVEN, A_Q_HEADS), 0.5),
        "ab_gk_f": nrm(ks[14], (N_EVEN, B_GATE_RANK, B_QK_W), B_GATE_RANK ** -0.5),
        "ab_gk_f_bias": nrm(ks[15], (N_EVEN, B_QK_W), 0.1),
        "ab_gk_b": nrm(ks[16], (N_EVEN, B_GATE_RANK, B_QK_W), B_GATE_RANK ** -0.5),
        "ab_gk_b_bias": nrm(ks[17], (N_EVEN, B_QK_W), 0.1),
        "ab_gla_norm": gain(ks[18], (N_EVEN, B_DV)),
        "ab_w_out": nrm(ks[19], (N_EVEN, MIX_OUT, D_MODEL), MIX_OUT ** -0.5),
        "c_w_out": nrm(ks[20], (N_ODD, D_MODEL, D_MODEL), D_MODEL ** -0.5),
        "c_b_out": nrm(ks[21], (N_ODD, D_MODEL), 0.02),
    }


def reference(x, c, ctx, c_ctx, ada_w, ada_b, norm_mix, norm_mlp, mlp_w1, mlp_w2,
              ab_w_in, ab_q_norm, ab_k_norm, ab_sink, ab_gk_f, ab_gk_f_bias, ab_gk_b, ab_gk_b_bias,
              ab_gla_norm, ab_w_out, c_w_out, c_b_out):
    n_lat = x.shape[1]
    rows = n_lat // GRID_W
    ang_r, ang_c = axial_rope_angles(rows)
    silu_c = jax.nn.silu(c)
    silu_cc = jax.nn.silu(c_ctx)
    for layer in range(DEPTH):
        ctx_live = any(k % 2 == 0 for k in range(layer + 1, DEPTH))
        sh1, sc1, g1, sh2, sc2, g2 = jnp.split((silu_c @ ada_w[layer] + ada_b[layer])[:, None, :], 6, axis=-1)
        csh1, csc1, cg1, csh2, csc2, cg2 = jnp.split(silu_cc @ ada_w[layer] + ada_b[layer], 6, axis=-1)
        h = modulate(rms_norm(x, norm_mix[layer]), sh1, sc1)
        if layer % 2 == 0:
            i = layer // 2
            hc = modulate(rms_norm(ctx, norm_mix[layer]), csh1, csc1)
            y, yc = ab_mixer(h, hc, ctx_live, ab_w_in[i], ab_q_norm[i], ab_k_norm[i], ab_sink[i],
                             ab_gk_f[i], ab_gk_f_bias[i], ab_gk_b[i], ab_gk_b_bias[i],
                             ab_gla_norm[i], ab_w_out[i], ang_r, ang_c)
            if ctx_live:
                ctx = ctx + cg1 * yc
        else:
            j = layer // 2
            y = fourier_mixer(h, c_w_out[j], c_b_out[j])
            if ctx_live:
                hc = modulate(rms_norm(ctx, norm_mix[layer]), csh1, csc1)
                ctx = ctx + cg1 * fourier_mixer(hc, c_w_out[j], c_b_out[j])
        x = x + g1 * y
        x = x + g2 * sq_relu_mlp(modulate(rms_norm(x, norm_mlp[layer]), sh2, sc2), mlp_w1[layer], mlp_w2[layer])
        if ctx_live:
            ctx = ctx + cg2 * sq_relu_mlp(modulate(rms_norm(ctx, norm_mlp[layer]), csh2, csc2),
                                          mlp_w1[layer], mlp_w2[layer])
    return x
```

```python
import numpy as np
import ml_dtypes
import concourse.bass as bass
import concourse.mybir as mybir
from concourse.bass_utils import run_bass_kernel_spmd
from contextlib import ExitStack

F32 = mybir.dt.float32
BF16 = mybir.dt.bfloat16
AF = mybir.ActivationFunctionType
ALU = mybir.AluOpType
AX = mybir.AxisListType

ENGS = ("pe", "act", "dve", "pool", "sp")
NDMASEM = 16
SAME_ENGINE_SYNC = True
EPS = 1e-6

L = 2048
CTX = 256
TT_ = L + CTX
D = 2048
DFF = 8192
INC = 4640


class Buf:
    __slots__ = ("name", "w", "rc", "rd")

    def __init__(self, name="", fence=None):
        self.name = name
        self.w = None
        if fence is None:
            self.rc = {}
            self.rd = []
        else:
            self.rc = dict(fence[0])
            self.rd = list(fence[1])


class Op:
    __slots__ = ("eng", "fn", "deps", "seq", "signal", "semval", "dma", "dslot", "dval", "clock")


class Prog:
    def __init__(self, nc):
        self.nc = nc
        self.stream = {e: [] for e in ENGS}
        self.known = {e: {f: -1 for f in ENGS} for e in ENGS}
        self.known_dma = {e: set() for e in ENGS}
        self.ndma = {e: 0 for e in ENGS}
        self.dma_hist = {e: [] for e in ENGS}
        self.last_c = {}
        self.fence_state = None

    def fence(self):
        rd = []
        for e in ENGS:
            rd.extend(self.dma_hist[e][-NDMASEM:])
        self.fence_state = (dict(self.last_c), rd)

    def buf(self, name=""):
        return Buf(name, self.fence_state)

    def op(self, eng, fn, reads=(), writes=(), dma=False):
        o = Op()
        o.eng = eng
        o.fn = fn
        o.dma = dma
        o.signal = False
        o.semval = 0
        st = self.stream[eng]
        o.seq = len(st)
        deps = []
        for b in reads:
            if b.w is not None:
                deps.append(b.w)
        for b in writes:
            if b.w is not None:
                deps.append(b.w)
            deps.extend(b.rc.values())
            deps.extend(b.rd)
        if dma:
            k = self.ndma[eng]
            self.ndma[eng] = k + 1
            o.dslot = k % NDMASEM
            o.dval = 16 * (k // NDMASEM + 1)
            hist = self.dma_hist[eng]
            if k >= NDMASEM:
                deps.append(hist[k - NDMASEM])
            hist.append(o)
        known = self.known[eng]
        kd = self.known_dma[eng]
        best = {}
        need = []
        for d in deps:
            if d.dma:
                if d in kd:
                    continue
                kd.add(d)
                need.append(d)
            else:
                if d.eng == eng and (eng == "pe" or not SAME_ENGINE_SYNC):
                    continue
                if known[d.eng] >= d.seq:
                    continue
                if d.eng not in best or best[d.eng].seq < d.seq:
                    best[d.eng] = d
        for d in best.values():
            d.signal = True
            need.append(d)
            for f in ENGS:
                if d.clock[f] > known[f]:
                    known[f] = d.clock[f]
            if d.seq > known[d.eng]:
                known[d.eng] = d.seq
        o.deps = need
        clk = dict(known)
        if not dma:
            clk[eng] = o.seq
            self.last_c[eng] = o
        o.clock = clk
        for b in reads:
            if dma:
                b.rd.append(o)
            else:
                b.rc[eng] = o
        for b in writes:
            b.w = o
            b.rc = {}
            b.rd = []
        st.append(o)
        return o

    def emit(self):
        nc = self.nc
        with ExitStack() as es:
            sems = {e: es.enter_context(nc.semaphore("s_" + e)) for e in ENGS}
            dsems = {}
            for e in ENGS:
                if self.ndma[e] > 0:
                    dsems[e] = [es.enter_context(nc.semaphore("d_%s_%d" % (e, i)))
                                for i in range(min(NDMASEM, self.ndma[e]))]
            finals = []
            for e in ENGS:
                for o in reversed(self.stream[e]):
                    if not o.dma:
                        o.signal = True
                        finals.append(o)
                        break
            for e in ENGS:
                c = 0
                for o in self.stream[e]:
                    if o.signal and not o.dma:
                        c += 1
                        o.semval = c
            alld = [o for e in ENGS for o in self.stream[e] if o.dma]

            def emit_waits(eobj, deps):
                for d in deps:
                    if d.dma:
                        eobj.wait_ge(dsems[d.eng][d.dslot], d.dval)
                    else:
                        eobj.wait_ge(sems[d.eng], d.semval)

            def run(ename, eobj):
                for o in self.stream[ename]:
                    emit_waits(eobj, o.deps)
                    ins = o.fn(eobj)
                    if o.dma:
                        ins.then_inc(dsems[ename][o.dslot], 16)
                    elif o.signal:
                        ins.then_inc(sems[ename], 1)
                if ename == "sp":
                    emit_waits(eobj, finals)
                    last = {}
                    for o in alld:
                        last[(o.eng, o.dslot)] = o
                    emit_waits(eobj, list(last.values()))

            with nc.Block() as block:
                @block.tensor
                def _(e):
                    run("pe", e)

                @block.scalar
                def _(e):
                    run("act", e)

                @block.vector
                def _(e):
                    run("dve", e)

                @block.gpsimd
                def _(e):
                    run("pool", e)

                @block.sync
                def _(e):
                    run("sp", e)


class Arena:
    def __init__(self, ap_f32, words, prog):
        self.base = ap_f32
        self.words = words
        self.top = 0
        self.marks = []
        self.prog = prog
        self.peak = 0

    def alloc(self, shape_free, dtype, parts=128):
        n = int(np.prod(shape_free))
        nbytes = n * (2 if dtype == BF16 else 4)
        w = (nbytes + 3) // 4
        w = (w + 7) // 8 * 8
        assert self.top + w <= self.words, ("SBUF arena overflow", self.top, w, self.words)
        v = self.base[0:parts, self.top:self.top + w]
        self.top += w
        self.peak = max(self.peak, self.top)
        if dtype == BF16:
            v = v.bitcast(BF16)[:, 0:n]
        else:
            v = v[:, 0:n]
        if len(shape_free) == 2:
            v = v.rearrange("p (a b) -> p a b", a=shape_free[0])
        elif len(shape_free) == 3:
            v = v.rearrange("p (a b c) -> p a b c", a=shape_free[0], b=shape_free[1])
        return v

    def mark(self):
        self.marks.append(self.top)

    def release(self):
        self.top = self.marks.pop()
        self.prog.fence()


class Rot:
    def __init__(self, ar, P, n, shape, dtype, name="rot", parts=128):
        self.t = [(ar.alloc(shape, dtype, parts), P.buf("%s%d" % (name, i))) for i in range(n)]
        self.i = 0

    def next(self):
        r = self.t[self.i % len(self.t)]
        self.i += 1
        return r


VC = dict(NM0=0, NM1=16, NL0=32, NL1=48, CB=64, AB0=80, AB1=176, QN=272, KN=273, GN=274, SK=276)
NVEC = 284


def build_program(dbg=(), stop_after=None):
    nc = bass.Bass("TRN2", target_bir_lowering=False)

    def din(name, shape, dt=F32):
        return nc.dram_tensor(name, shape, dt, kind="ExternalInput").ap()

    def dscr(name, shape, dt=F32):
        kind = "ExternalOutput" if name in dbg else "Internal"
        return nc.dram_tensor(name, shape, dt, kind=kind).ap()

    x_d = din("x", [L, D])
    ctx_d = din("ctx", [CTX, D])
    cc_d = din("cc", [128, 32])
    vec_d = din("vec", [128, NVEC])
    gk_d = din("gk", [2, 17, 512])
    gnrow_d = din("gnrow", [1, 256])
    adaw_d = din("ada_w", [2, D, 6 * D])
    win_d = din("w_in", [D, INC])
    wout_d = din("w_out", [D, D])
    cw_d = din("c_w", [D, D])
    w1_d = din("w1", [2, D, DFF])
    w2_d = din("w2", [2, DFF, D])
    ident_d = din("ident", [128, 128])
    rt_d = din("rt", [128, 128])
    amask_d = din("amask", [128, 2, 128])
    tri_d = din("tri", [128, 6, 128])
    cos_d = din("cosT", [128, L])
    sin_d = din("sinT", [128, L])
    dft256_d = din("dft256", [256, 512])
    dftL_d = din("dftL", [2, L, 1040], BF16)
    out_d = nc.dram_tensor("out", [L, D], F32, kind="ExternalOutput").ap()

    xT_d = dscr("xT", [D, TT_])
    qT_d = dscr("qT", [8, 128, L], BF16)
    kT_d = dscr("kT", [2, 128, TT_], BF16)
    vA_d = dscr("vA", [TT_, 256], BF16)
    bqT_d = dscr("bqT", [4, 128, L])
    bkT_d = dscr("bkT", [4, 128, TT_])
    bktok_d = dscr("bktok", [TT_, 512])
    bv_d = dscr("bv", [TT_, 1024], BF16)
    gateT_d = dscr("gateT", [1024, L])
    lrT_d = dscr("lrT", [2, 16, TT_])
    catT_d = dscr("catT", [D, L], BF16)
    U_d = dscr("U", [L, 8, 512], BF16)
    fT_d = dscr("fT", [D, L], BF16)
    modT_d = dscr("modT", [2, 128, 192])
    hT_d = dscr("hT", [D, TT_], BF16)

    es = ExitStack()
    AW = 53200
    sb = es.enter_context(nc.sbuf_tensor("arena", [128, AW], F32))
    PS = [es.enter_context(nc.psum_tensor("ps%d" % i, [128, 512], F32)) for i in range(8)]
    P = Prog(nc)
    ar = Arena(sb[:], AW, P)
    PSB = [Buf("psb%d" % i) for i in range(8)]
    psc = [0]

    ps_ids = [list(range(8))]

    def next_ps():
        ids = ps_ids[0]
        i = ids[psc[0] % len(ids)]
        psc[0] += 1
        return PS[i][:], PSB[i]

    def pick_ps(i):
        return PS[i][:], PSB[i]

    def DMA(eng, out, in_, r=(), w=()):
        return P.op(eng, lambda e: e.dma_start(out=out, in_=in_), r, w, dma=True)

    def MM(out, lhsT, rhs, start, stop, r, w):
        return P.op("pe", lambda e: e.matmul(out, lhsT=lhsT, rhs=rhs, start=start, stop=stop), r, w)

    def TR(out, in_, ident, r, w):
        return P.op("pe", lambda e: e.transpose(out, in_, ident), r, w)

    def ACT(out, in_, func, r, w, bias=None, scale=None, accum_out=None):
        kw = {}
        if bias is not None:
            kw["bias"] = bias
        if scale is not None:
            kw["scale"] = scale
        if accum_out is not None:
            kw["accum_out"] = accum_out
        return P.op("act", lambda e: e.activation(out, in_, func, **kw), r, w)

    def TTo(eng, out, in0, in1, op, r, w):
        return P.op(eng, lambda e: e.tensor_tensor(out, in0, in1, op), r, w)

    def STT(eng, out, in0, scalar, in1, op0, op1, r, w):
        return P.op(eng, lambda e: e.scalar_tensor_tensor(out, in0=in0, scalar=scalar, in1=in1, op0=op0, op1=op1), r, w)

    def TS(eng, out, in0, s1, s2, op0, op1, r, w):
        if op1 is None:
            return P.op(eng, lambda e: e.tensor_scalar(out, in0, s1, None, op0=op0), r, w)
        return P.op(eng, lambda e: e.tensor_scalar(out, in0, s1, s2, op0=op0, op1=op1), r, w)

    def CP(eng, out, in_, r, w):
        if eng == "act":
            return P.op("act", lambda e: e.copy(out, in_), r, w)
        return P.op(eng, lambda e: e.tensor_copy(out, in_), r, w)

    def RECIP(out, in_, r, w):
        return P.op("dve", lambda e: e.reciprocal(out, in_), r, w)

    def MEMSET(eng, out, val, w):
        return P.op(eng, lambda e: e.memset(out, val), (), w)

    def pipeline(items, pre, body, la=2):
        n = len(items)
        if pre is not None:
            for i in range(min(la, n)):
                pre(items[i])
        for i in range(n):
            if pre is not None and i + la < n:
                pre(items[i + la])
            body(items[i])

    ident = ar.alloc([128], F32)
    identB = P.buf("ident")
    DMA("sp", ident, ident_d, (), [identB])
    ones32 = ar.alloc([128], F32)
    onesB = P.buf("ones")
    MEMSET("dve", ones32, 1.0, [onesB])
    ones16 = ar.alloc([128], BF16)
    MEMSET("dve", ones16, 1.0, [onesB])
    vec = ar.alloc([NVEC], F32)
    vecB = P.buf("vec")
    DMA("sp", vec, vec_d, (), [vecB])
    mv = [ar.alloc([7, 16], F32) for _ in range(2)]
    cmv = ar.alloc([2, 16], F32)
    mvB = P.buf("mv")
    epsc = ar.alloc([1], F32)
    MEMSET("dve", epsc, EPS, [onesB])

    XB = [[Buf("xT_%d_%d" % (j, g)) for g in range(5)] for j in range(16)]

    def xgrp(g):
        return (g * 512, 512) if g < 4 else (L, CTX)

    xT_v = xT_d.rearrange("(j p) t -> p j t", p=128)

    done = [False]

    def finish():
        P.emit()
        es.close()
        return nc

    cc = ar.alloc([32], F32)
    ccB = P.buf("cc")
    DMA("sp", cc, cc_d, (), [ccB])
    ACT(cc, cc, AF.Silu, [ccB], [ccB])
    sccb = ar.alloc([2, 16], BF16)
    CP("dve", sccb.rearrange("p a b -> p (a b)"), cc, [ccB], [ccB])
    modT = ar.alloc([2, 192], F32)
    mtB = P.buf("modT")

    def mods_gen(l, w=512, nb0=0, nb1=None, fin="ab"):
        wrot = Rot(ar, P, 2, [16, w], BF16, "adaw")
        rrot = Rot(ar, P, 2, [w], F32, "modrow", parts=2)
        nj = w // 128
        for nb in range(nb0, 6 * D // w if nb1 is None else nb1):
            wt, wB = wrot.next()
            DMA("pool", wt, adaw_d[l, :, nb * w:(nb + 1) * w].rearrange("(j p) n -> p j n", p=128), (), [wB])
            ps, pB = next_ps()
            for j in range(16):
                MM(ps[0:2, 0:w], sccb[:, :, j], wt[:, j, :], j == 0, j == 15, [ccB, wB], [pB])
            row, rB = rrot.next()
            CP("act", row[0:2, :], ps[0:2, 0:w], [pB], [rB])
            ps2, p2B = next_ps()
            for jj in range(nj):
                TR(ps2[:, 2 * jj:2 * jj + 2], row[0:2, jj * 128:(jj + 1) * 128], ident[0:2, 0:2], [rB, identB], [p2B])
            ab = vec[:, VC["AB0"] + 96 * l + nb * nj: VC["AB0"] + 96 * l + (nb + 1) * nj]
            for r_ in range(2):
                TTo("dve", modT[:, l, nb * 2 * nj:(nb + 1) * 2 * nj].rearrange("p (j r) -> p j r", r=2)[:, :, r_],
                    ps2[:, 0:2 * nj].rearrange("p (j r) -> p j r", r=2)[:, :, r_], ab, ALU.add, [p2B, vecB], [mtB])
            yield
        m3 = modT[:, l, :].rearrange("p (j r) -> p j r", r=2)
        nm = vec[:, VC["NM0"] + 16 * l: VC["NM0"] + 16 * (l + 1)]
        nl = vec[:, VC["NL0"] + 16 * l: VC["NL0"] + 16 * (l + 1)]
        if "a" in fin:
            CP("dve", mv[l][:, 0, :], m3[:, 0:16, 0], [mtB], [mvB])
            STT("dve", mv[l][:, 1, :], m3[:, 16:32, 0], 1.0, nm, ALU.add, ALU.mult, [mtB, vecB], [mvB])
            if l == 0:
                CP("dve", cmv[:, 0, :], m3[:, 0:16, 1], [mtB], [mvB])
                STT("dve", cmv[:, 1, :], m3[:, 16:32, 1], 1.0, vec[:, VC["NM0"]:VC["NM0"] + 16], ALU.add, ALU.mult,
                    [mtB, vecB], [mvB])
        if "b" in fin:
            CP("dve", mv[l][:, 2, :], m3[:, 32:48, 0], [mtB], [mvB])
            CP("dve", mv[l][:, 3, :], m3[:, 48:64, 0], [mtB], [mvB])
            STT("dve", mv[l][:, 4, :], m3[:, 64:80, 0], 1.0, nl, ALU.add, ALU.mult, [mtB, vecB], [mvB])
            CP("dve", mv[l][:, 5, :], m3[:, 80:96, 0], [mtB], [mvB])
            TTo("dve", mv[l][:, 6, :], m3[:, 32:48, 0], vec[:, VC["CB"]:VC["CB"] + 16], ALU.mult, [mtB, vecB], [mvB])
        yield

    def drain(gen):
        if gen is not None:
            for _ in gen:
                pass

    def phase_transpose_in():
        ar.mark()
        bg = mods_gen(0, 512, 0, 8, "a")
        rin = Rot(ar, P, 2, [4, D], F32, "xin")
        rout = Rot(ar, P, 2, [16, 512], F32, "xstg")
        cur = {}

        def pre(g):
            xin, xB = rin.next()
            if g < 4:
                DMA("sp", xin, x_d[g * 512:(g + 1) * 512, :].rearrange("(i p) d -> p i d", p=128), (), [xB])
            else:
                DMA("sp", xin[:, 0:2, :], ctx_d.rearrange("(i p) d -> p i d", p=128), (), [xB])
            cur[g] = (xin, xB)

        def body(g):
            xin, xB = cur.pop(g)
            t0, n = xgrp(g)
            nt = n // 128
            stg, sB = rout.next()
            for j in range(16):
                ps, pB = next_ps()
                for i in range(nt):
                    TR(ps[:, i * 128:(i + 1) * 128], xin[:, i, j * 128:(j + 1) * 128], ident, [xB, identB], [pB])
                CP("act" if j % 2 else "dve", stg[:, j, 0:n], ps[:, 0:n], [pB], [sB])
                if j % 3 == 0:
                    next(bg, None)
            DMA("sp", xT_v[:, :, t0:t0 + n], stg[:, :, 0:n], [sB], [XB[j][g] for j in range(16)])

        pipeline(list(range(5)), pre, body, la=1)
        drain(bg)
        ar.release()

    def red(out, in_, r, w):
        return P.op("dve", lambda e: e.tensor_reduce(out, in_, AX.X, ALU.add), r, w)

    def norm_phase(groups, hT, hB_of, scal_of):
        ar.mark()
        rx = Rot(ar, P, 2, [16, 512], F32, "nx")
        sqr = Rot(ar, P, 2, [4, 512], F32, "nsq")
        ssum = ar.alloc([4, 512], F32)
        ssB = P.buf("nss")
        rstd = ar.alloc([512], F32)
        rsB = P.buf("nrs")
        rtmp = Rot(ar, P, 3, [512], F32, "ntmp")
        cur = {}

        def pre(it):
            g, off = it
            t0, n = xgrp(g)
            xg, xB = rx.next()
            DMA("sp", xg[:, :, 0:n], xT_v[:, :, t0:t0 + n], [XB[j][g] for j in range(16)], [xB])
            cur[g] = (xg, xB)

        def body(it):
            g, off = it
            t0, n = xgrp(g)
            xg, xB = cur.pop(g)
            sh, gsc = scal_of(g)
            ps, pB = next_ps()
            for q_ in range(4):
                sq, sqB = sqr.next()
                ACT(sq[:, :, 0:n], xg[:, 4 * q_:4 * q_ + 4, 0:n], AF.Square, [xB], [sqB])
                red(ssum[:, q_, 0:n], sq[:, :, 0:n].rearrange("p j t -> p t j"), [sqB], [ssB])
                MM(ps[:, 0:n], ones32, ssum[:, q_, 0:n], q_ == 0, q_ == 3, [onesB, ssB], [pB])
            ACT(rstd[:, 0:n], ps[:, 0:n], AF.Sqrt, [pB, onesB], [rsB], bias=epsc, scale=1.0 / D)
            RECIP(rstd[:, 0:n], rstd[:, 0:n], [rsB], [rsB])
            for j in range(16):
                tmp, tB = rtmp.next()
                STT("dve", tmp[:, 0:n], xg[:, j, 0:n], gsc[:, j:j + 1], rstd[:, 0:n], ALU.mult, ALU.mult,
                    [xB, mvB, rsB], [tB])
                ACT(hT[:, j, off:off + n], tmp[:, 0:n], AF.Identity, [tB, mvB], [hB_of(g)], bias=sh[:, j:j + 1])

        pipeline(groups, pre, body, la=1)
        ar.release()

    WROT = Rot(ar, P, 2, [8192], BF16, "W")

    class WView:
        def __init__(self, rot, a):
            self.rot, self.a = rot, a

        def next(self):
            t, B_ = self.rot.next()
            return t.rearrange("p (a b) -> p a b", a=self.a), B_

    def run(gen):
        for _ in gen:
            pass

    def chain(*gens):
        for g_ in gens:
            for _ in g_:
                yield

    def interleave(ga, na, gb_, nb):
        da = db = False
        while not (da and db):
            for _ in range(na):
                if not da:
                    try:
                        next(ga)
                    except StopIteration:
                        da = True
            for _ in range(nb):
                if not db:
                    try:
                        next(gb_)
                    except StopIteration:
                        db = True

    def gemm_fm_gen(W, row0, KC, col0, ncols, act, actB_of, groups, epi, wrot, bw=512, mch=128, pre=None, la=2, psfn=None):
        items = []
        for c0 in range(0, ncols, bw):
            cw = min(bw, ncols - c0)
            for m0 in range(0, cw, mch):
                for gi in groups:
                    items.append((c0, cw, m0, gi))
        state = {"c0": None}
        nit = len(items)
        if pre is not None:
            for i in range(min(la, nit)):
                pre(items[i])
        for i, it in enumerate(items):
            if pre is not None and i + la < nit:
                pre(items[i + la])
            c0, cw, m0, (gid, aoff, n) = it
            if state["c0"] != c0:
                wt, wB = wrot.next()
                DMA("pool", wt[:, 0:KC, 0:cw],
                    W[row0:row0 + KC * 128, col0 + c0:col0 + c0 + cw].rearrange("(j p) n -> p j n", p=128), (), [wB])
                state["c0"] = c0
                state["w"] = (wt, wB)
            wt, wB = state["w"]
            m = min(mch, cw - m0)
            ps, pB = (psfn or next_ps)()
            for j in range(KC):
                MM(ps[0:m, 0:n], wt[:, j, m0:m0 + m], act[:, j, aoff:aoff + n], j == 0, j == KC - 1,
                   [wB, actB_of(gid)], [pB])
            epi((c0 + m0) // mch, gid, ps, pB, m, n)
            yield

    def gemm_fm(*a_, **k_):
        run(gemm_fm_gen(*a_, **k_))

    def gemm_tm_gen(W, row0, KC, col0, ncols, act, actB_of, tiles, epi, wrot):
        for c0 in range(0, ncols, 512):
            cw = min(512, ncols - c0)
            wt, wB = wrot.next()
            DMA("pool", wt[:, 0:KC, 0:cw],
                W[row0:row0 + KC * 128, col0 + c0:col0 + c0 + cw].rearrange("(j p) n -> p j n", p=128), (), [wB])
            for (ti, aoff, gid) in tiles:
                ps, pB = next_ps()
                for j in range(KC):
                    MM(ps[:, 0:cw], act[:, j, aoff:aoff + 128], wt[:, j, 0:cw], j == 0, j == KC - 1,
                       [wB, actB_of(gid)], [pB])
                epi(c0, cw, ti, ps, pB)
                yield

    qTB = Buf("qT")
    kTB = Buf("kT")
    vAB = Buf("vA")
    bqTB = Buf("bqT")
    bkTB = Buf("bkT")
    bktokB = Buf("bktok")
    bvB = Buf("bv")
    gateTB = Buf("gateT")
    lrTB = Buf("lrT")

    CTB = [Buf("catT%d" % i) for i in range(16)]
    catT_v = catT_d.rearrange("(j p) t -> p j t", p=128)

    def phase_inproj():
        ar.mark()
        hT = ar.alloc([16, TT_], BF16)
        hB = [P.buf("hT%d" % g) for g in range(5)]
        norm_phase([(g, xgrp(g)[0]) for g in range(5)], hT, lambda g: hB[g],
                   lambda g: (mv[0][:, 0, :], mv[0][:, 1, :]) if g < 4 else (cmv[:, 0, :], cmv[:, 1, :]))
        if "hT" in dbg:
            DMA("sp", hT_d.rearrange("(j p) t -> p j t", p=128), hT, hB, [Buf()])
        wA = WView(WROT, 16)
        wB_ = WView(Rot(ar, P, 2, [8192], BF16, "winB"), 16)
        o16 = Rot(ar, P, 3, [512], BF16, "o16")
        o32 = Rot(ar, P, 3, [512], F32, "o32")
        lat = [(g, g * 512, 512) for g in range(4)]
        allg = lat + [(4, L, CTX)]
        hBf = lambda g: hB[g]
        tiles_all = [(i, i * 128, (i // 4) if i < 16 else 4) for i in range(18)]
        ar.mark()
        cosT = ar.alloc([L], F32)
        sinT = ar.alloc([L], F32)
        rt32 = ar.alloc([128], F32)
        cB = P.buf("cossin")
        DMA("sp", cosT, cos_d, (), [cB])
        DMA("sp", sinT, sin_d, (), [cB])
        DMA("sp", rt32, rt_d, (), [cB])
        r_sq = Rot(ar, P, 3, [512], F32, "qsq")
        r_rs = Rot(ar, P, 2, [512], F32, "qrs")
        r_qn = Rot(ar, P, 3, [512], F32, "qqn")
        r_t1 = Rot(ar, P, 2, [512], F32, "qt1")
        r_t2 = Rot(ar, P, 2, [512], F32, "qt2")
        deferred = []
        hctr = [0]
        ectr = [0]

        def ps_heavy():
            hctr[0] += 1
            return pick_ps(hctr[0] % 2)

        def ps_epi():
            ectr[0] += 1
            return pick_ps(2 + ectr[0] % 2)

        def epi_qk(dst, dstB, gvec):
            def epi(mi, gid, ps, pB, m, n):
                t0 = xgrp(gid)[0]
                sq, sB = r_sq.next()
                ACT(sq[:, 0:n], ps[:, 0:n], AF.Square, [pB], [sB])

                def stage2():
                    ps2, p2B = ps_epi()
                    MM(ps2[:, 0:n], ones32, sq[:, 0:n], True, True, [onesB, sB], [p2B])
                    rs, rB = r_rs.next()
                    ACT(rs[:, 0:n], ps2[:, 0:n], AF.Sqrt, [p2B, onesB], [rB], bias=epsc, scale=1.0 / 128)
                    RECIP(rs[:, 0:n], rs[:, 0:n], [rB], [rB])
                    qn, qB = r_qn.next()
                    STT("dve", qn[:, 0:n], ps[:, 0:n], gvec, rs[:, 0:n], ALU.mult, ALU.mult, [pB, vecB, rB], [qB])
                    if gid < 4:
                        def stage3():
                            ot, oB = o16.next()
                            ps3, p3B = ps_epi()
                            MM(ps3[:, 0:n], rt32, qn[:, 0:n], True, True, [cB, qB], [p3B])
                            t1, t1B = r_t1.next()
                            TTo("dve", t1[:, 0:n], qn[:, 0:n], cosT[:, t0:t0 + n], ALU.mult, [qB, cB], [t1B])
                            t2, t2B = r_t2.next()
                            TTo("dve", t2[:, 0:n], ps3[:, 0:n], sinT[:, t0:t0 + n], ALU.mult, [p3B, cB], [t2B])
                            TTo("dve", ot[:, 0:n], t1[:, 0:n], t2[:, 0:n], ALU.add, [t1B, t2B], [oB])
                            DMA("sp", dst[mi, :, t0:t0 + n], ot[:, 0:n], [oB], [dstB])
                        deferred.append(stage3)
                    else:
                        ot, oB = o16.next()
                        CP("dve", ot[:, 0:n], qn[:, 0:n], [qB], [oB])
                        DMA("sp", dst[mi, :, t0:t0 + n], ot[:, 0:n], [oB], [dstB])
                deferred.append(stage2)
            return epi

        def epi_v(dst, dstB, rot):
            def epi(c0, cw, ti, ps, pB):
                ot, oB = rot.next()
                CP("act" if ti % 2 else "dve", ot[:, 0:cw], ps[:, 0:cw], [pB], [oB])
                DMA("sp", dst[ti * 128:(ti + 1) * 128, c0:c0 + cw], ot[:, 0:cw], [oB], [dstB])
            return epi

        def epi_bq(mi, gid, ps, pB, m, n):
            t0 = xgrp(gid)[0]
            ot, oB = o32.next()
            ACT(ot[:, 0:n], ps[:, 0:n], AF.Identity, [pB], [oB], scale=float(128 ** -0.5))
            DMA("sp", bqT_d[mi, :, t0:t0 + n], ot[:, 0:n], [oB], [bqTB])

        def epi_bk(mi, gid, ps, pB, m, n):
            t0 = xgrp(gid)[0]
            ot, oB = o32.next()
            CP("dve", ot[:, 0:n], ps[:, 0:n], [pB], [oB])
            DMA("sp", bkT_d[mi, :, t0:t0 + n], ot[:, 0:n], [oB], [bkTB])

        def epi_gate(mi, gid, ps, pB, m, n):
            t0 = xgrp(gid)[0]
            ot, oB = o32.next()
            ACT(ot[:, 0:n], ps[:, 0:n], AF.Silu, [pB], [oB])
            DMA("sp", gateT_d[mi * 128:(mi + 1) * 128, t0:t0 + n], ot[:, 0:n], [oB], [gateTB])

        def epi_lr(mi, gid, ps, pB, m, n):
            t0 = xgrp(gid)[0]
            ot, oB = o32.next()
            CP("dve", ot[0:16, 0:n], ps[0:16, 0:n], [pB], [oB])
            DMA("sp", lrT_d[mi, :, t0:t0 + n], ot[0:16, 0:n], [oB], [lrTB])

        genA = chain(
            gemm_fm_gen(win_d, 0, 16, 0, 1024, hT, hBf, lat, epi_qk(qT_d, qTB, vec[:, VC["QN"]:VC["QN"] + 1]), wA,
                        psfn=ps_heavy),
            gemm_fm_gen(win_d, 0, 16, 1024, 256, hT, hBf, allg, epi_qk(kT_d, kTB, vec[:, VC["KN"]:VC["KN"] + 1]), wA,
                        psfn=ps_heavy),
            gemm_tm_gen(win_d, 0, 16, 1280, 256, hT, hBf, tiles_all, epi_v(vA_d, vAB, o16), wA))
        genB = chain(
            gemm_fm_gen(win_d, 0, 16, 1536, 512, hT, hBf, lat, epi_bq, wB_),
            gemm_fm_gen(win_d, 0, 16, 2048, 512, hT, hBf, allg, epi_bk, wB_),
            gemm_fm_gen(win_d, 0, 16, 4608, 32, hT, hBf, allg, epi_lr, wB_, bw=32, mch=16),
            gemm_tm_gen(win_d, 0, 16, 2560, 1024, hT, hBf, tiles_all, epi_v(bv_d, bvB, o16), wB_),
            gemm_tm_gen(win_d, 0, 16, 2048, 512, hT, hBf, tiles_all, epi_v(bktok_d, bktokB, o32), wB_),
            gemm_fm_gen(win_d, 0, 16, 3584, 1024, hT, hBf, lat, epi_gate, wB_))
        ps_ids[0] = [4, 5, 6, 7]

        def run_deferred():
            d_ = deferred[:]
            del deferred[:]
            for f_ in d_:
                f_()
        doneA = False
        while not doneA or deferred:
            if not doneA:
                try:
                    next(genA)
                except StopIteration:
                    doneA = True
            next(genB, None)
            run_deferred()
            next(genB, None)
            run_deferred()
        ps_ids[0] = list(range(8))
        run(genB)
        ar.release()
        ar.release()

    def phase_attn():
        ar.mark()
        bg2 = mods_gen(0, 512, 8, 24, "b")
        ps_ids[0] = [7]
        for _ in attn_gen():
            next(bg2, None)
        drain(bg2)
        ps_ids[0] = list(range(8))
        ar.release()

    def attn_gen():
        am = ar.alloc([2, 128], BF16)
        amB = P.buf("amask")
        DMA("pool", am, amask_d, (), [amB])
        esink = ar.alloc([8], F32)
        esB = P.buf("esink")
        ACT(esink, vec[:, VC["SK"]:VC["SK"] + 8], AF.Exp, [vecB], [esB])
        kT = ar.alloc([TT_], BF16)
        vS = ar.alloc([18, 128], BF16)
        qS = ar.alloc([4, L], BF16)
        kvB = P.buf("kv")
        prot = Rot(ar, P, 6, [4, 128], BF16, "pT")
        orot = Rot(ar, P, 3, [4, 128], BF16, "aout")
        den = ar.alloc([4, 128], F32)
        denB = P.buf("den")
        scale = float(128 ** -0.5)
        sctr = [0]
        for kv in range(2):
            DMA("sp", kT, kT_d[kv], [kTB], [kvB])
            DMA("sp", vS, vA_d[:, kv * 128:(kv + 1) * 128].rearrange("(c p) d -> p c d", p=128), [vAB], [kvB])
            DMA("sp", qS, qT_d[kv * 4:(kv + 1) * 4].rearrange("h p t -> p h t"), [qTB], [kvB])
            for n in range(16):
                blocks = []
                if n > 0:
                    blocks.append((n - 1, 0))
                blocks.append((n, None))
                if n < 15:
                    blocks.append((n + 1, 1))
                blocks += [(16, None), (17, None)]
                psO, pOB = pick_ps(0)
                psD, pDB = pick_ps(1)
                nb_ = len(blocks)
                sps = []
                for bi, (kc, mk) in enumerate(blocks):
                    psS, pSB = pick_ps(2 + bi)
                    MM(psS.rearrange("p (h q) -> p h q", h=4), kT[:, kc * 128:(kc + 1) * 128],
                       qS[:, :, n * 128:(n + 1) * 128], True, True, [kvB], [pSB])
                    sps.append((psS, pSB))
                for bi, (kc, mk) in enumerate(blocks):
                    psS, pSB = sps[bi]
                    pT, pB_ = prot.next()
                    ACT(pT, psS.rearrange("p (h q) -> p h q", h=4), AF.Exp, [pSB], [pB_], scale=scale)
                    if mk is not None:
                        TTo("dve", pT, pT, am[:, mk, :].unsqueeze(1).to_broadcast([128, 4, 128]), ALU.mult,
                            [pB_, amB], [pB_])
                    MM(psO.rearrange("p (h q) -> p h q", h=4), vS[:, kc, :], pT, bi == 0, bi == nb_ - 1, [kvB, pB_], [pOB])
                    MM(psD.rearrange("p (h q) -> p h q", h=4), ones16, pT, bi == 0, bi == nb_ - 1, [onesB, pB_], [pDB])
                TTo("dve", den, psD.rearrange("p (h q) -> p h q", h=4),
                    esink[:, kv * 4:(kv + 1) * 4].unsqueeze(2).to_broadcast([128, 4, 128]), ALU.add, [pDB, esB], [denB])
                RECIP(den, den, [denB], [denB])
                ot, oB = orot.next()
                TTo("dve", ot, psO.rearrange("p (h q) -> p h q", h=4), den, ALU.mult, [pOB, denB], [oB])
                DMA("sp", catT_v[:, kv * 4:(kv + 1) * 4, n * 128:(n + 1) * 128], ot, [oB],
                    [CTB[kv * 4 + h] for h in range(4)])
                yield

    def phase_gla():
        ar.mark()
        bg = mods_gen(1, 512)
        gctr = [0]
        gout = Rot(ar, P, 3, [2, 128], BF16, "gout")
        tri = ar.alloc([6, 128], F32)
        triB = P.buf("tri")
        DMA("sp", tri, tri_d, (), [triB])
        gk = ar.alloc([2, 512], F32, parts=17)
        DMA("sp", gk, gk_d.rearrange("r k n -> k r n"), (), [triB])
        lr = ar.alloc([2, TT_], F32, parts=17)
        lrB = P.buf("lr")
        MEMSET("dve", lr[0:17, :, :], 1.0, [lrB])
        DMA("sp", lr[0:16, :, :], lrT_d.rearrange("r k t -> k r t"), [lrTB], [lrB])
        gnv = vec[:, VC["GN"]:VC["GN"] + 2]
        bq = ar.alloc([L], F32)
        bk = ar.alloc([TT_], F32)
        bkt = ar.alloc([18, 128], F32)
        vS = ar.alloc([18, 256], BF16)
        gt = ar.alloc([2, L], F32)
        inB = P.buf("glain")
        qe = [ar.alloc([L], BF16) for _ in range(2)]
        ke = [ar.alloc([L], BF16) for _ in range(2)]
        kd = [ar.alloc([18, 128], BF16) for _ in range(2)]
        dec = [ar.alloc([18], F32) for _ in range(2)]
        gB = [[P.buf("gla_g%d_%d" % (d, c)) for c in range(18)] for d in range(2)]
        S = [ar.alloc([256], F32) for _ in range(2)]
        SB_ = [P.buf("S%d" % d) for d in range(2)]
        Sfb = ar.alloc([16, 256], BF16)
        SfbB = [P.buf("Sfb%d" % i) for i in range(16)]
        Sbb = Rot(ar, P, 3, [256], BF16, "Sbb")
        r_e1 = Rot(ar, P, 2, [128], F32, "ge1")
        r_sp = Rot(ar, P, 3, [128], F32, "gsp")
        r_eb = Rot(ar, P, 2, [128], F32, "geb")
        r_enb = Rot(ar, P, 2, [128], F32, "genb")
        r_eD = Rot(ar, P, 2, [128], F32, "geD")
        r_a1 = Rot(ar, P, 2, [128], F32, "ga1")
        r_a2 = Rot(ar, P, 2, [128], F32, "ga2")
        at16 = Rot(ar, P, 3, [128], BF16, "AT")
        sq = Rot(ar, P, 3, [256], F32, "gsq")
        rsr = Rot(ar, P, 2, [128], F32, "grs")
        o32 = Rot(ar, P, 3, [128], F32, "go32")
        zctr = [0]
        xctr = [0]
        ps_ids[0] = [6, 7]
        LA = 3
        for hb in range(4):
            DMA("sp", bq, bqT_d[hb], [bqTB], [inB])
            DMA("sp", bk, bkT_d[hb], [bkTB], [inB])
            DMA("sp", bkt, bktok_d[:, hb * 128:(hb + 1) * 128].rearrange("(c p) d -> p c d", p=128), [bktokB], [inB])
            DMA("sp", vS, bv_d[:, hb * 256:(hb + 1) * 256].rearrange("(c p) e -> p c e", p=128), [bvB], [inB])
            DMA("sp", gt, gateT_d[hb * 256:(hb + 1) * 256, :].rearrange("(a p) t -> p a t", p=128), [gateTB], [inB])
            spm = {}

            def gates_a(d_, c):
                gctr[0] += 1
                if gctr[0] % 5 == 0:
                    next(bg, None)
                ts_ = slice(c * 128, (c + 1) * 128)
                psz, pzB = pick_ps(zctr[0] % 2)
                zctr[0] += 1
                MM(psz[:, 0:128], lr[0:17, d_, ts_], gk[0:17, d_, hb * 128:(hb + 1) * 128], True, True, [lrB, triB], [pzB])
                e1, e1B = r_e1.next()
                ACT(e1, psz[:, 0:128], AF.Exp, [pzB], [e1B], scale=-1.0)
                sp_, spB = r_sp.next()
                ACT(sp_, e1, AF.Ln, [e1B, onesB], [spB], bias=ones32[:, 0:1])
                spm[(d_, c)] = (sp_, spB, psz, pzB)

            def gates_b(d_, c):
                ts_ = slice(c * 128, (c + 1) * 128)
                G = gB[d_][c]
                sp_, spB, psz, pzB = spm.pop((d_, c))
                MM(psz[:, 128:256], sp_, tri[:, d_, :], True, True, [spB, triB], [pzB])
                MM(psz[:, 256:384], tri[:, 2 + d_, :], sp_, True, True, [spB, triB], [pzB])
                eb, ebB = r_eb.next()
                ACT(eb, psz[:, 128:256], AF.Exp, [pzB], [ebB])
                col = 127 if d_ == 0 else 0
                CP("dve", dec[d_][:, c:c + 1], eb[:, col:col + 1], [ebB], [G])
                if c < 16:
                    TTo("dve", qe[d_][:, ts_], bq[:, ts_], eb, ALU.mult, [inB, ebB], [G])
                    enb, enB = r_enb.next()
                    ACT(enb, psz[:, 128:256], AF.Exp, [pzB], [enB], scale=-1.0)
                    TTo("dve", ke[d_][:, ts_], bk[:, ts_], enb, ALU.mult, [inB, enB], [G])
                eD, eDB = r_eD.next()
                ACT(eD, psz[:, 256:384], AF.Exp, [pzB], [eDB])
                TTo("dve", kd[d_][:, c, :], bkt[:, c, :], eD, ALU.mult, [inB, eDB], [G])

            def upd(d_, c, first):
                ps, pB = pick_ps(5)
                MM(ps[:, 0:256], kd[d_][:, c, :], vS[:, c, :], True, True, [gB[d_][c], inB], [pB])
                if first:
                    CP("dve", S[d_], ps[:, 0:256], [pB], [SB_[d_]])
                else:
                    STT("dve", S[d_], S[d_], dec[d_][:, c:c + 1], ps[:, 0:256], ALU.mult, ALU.add,
                        [SB_[d_], gB[d_][c], pB], [SB_[d_]])

            def prologue(d_, order):
                for k in range(LA + 1):
                    gates_a(d_, order[k])
                    if k > 0:
                        gates_b(d_, order[k - 1])

            def lookahead(d_, order, i):
                if i + LA + 1 < len(order):
                    gates_a(d_, order[i + LA + 1])
                if i + LA < len(order):
                    gates_b(d_, order[i + LA])

            stm = {}

            def step_a(n):
                ts_ = slice(n * 128, (n + 1) * 128)
                G0, G1 = gB[0][n], gB[1][n]
                sbb, sbbB = Sbb.next()
                CP("act", sbb, S[1], [SB_[1]], [sbbB])
                psX, pXB = pick_ps(2 + xctr[0] % 3)
                xctr[0] += 1
                MM(psX[:, 0:128], ke[0][:, ts_], qe[0][:, ts_], True, True, [G0], [pXB])
                MM(psX[:, 128:256], ke[1][:, ts_], qe[1][:, ts_], True, True, [G1], [pXB])
                a1, a1B = r_a1.next()
                TTo("dve", a1, psX[:, 0:128], tri[:, 4, :], ALU.mult, [pXB, triB], [a1B])
                a2, a2B = r_a2.next()
                TTo("dve", a2, psX[:, 128:256], tri[:, 5, :], ALU.mult, [pXB, triB], [a2B])
                AT, atB = at16.next()
                TTo("dve", AT, a1, a2, ALU.add, [a1B, a2B], [atB])
                stm[n] = dict(psX=psX, pXB=pXB, AT=AT, atB=atB, sbb=sbb, sbbB=sbbB)

            def step_b(n):
                ts_ = slice(n * 128, (n + 1) * 128)
                G0, G1 = gB[0][n], gB[1][n]
                st_ = stm[n]
                psX, pXB = st_["psX"], st_["pXB"]
                psO = psX[:, 256:512]
                for hf in range(2):
                    es_ = slice(hf * 128, (hf + 1) * 128)
                    MM(psO[:, es_], vS[:, n, es_], st_["AT"], True, False, [inB, st_["atB"]], [pXB])
                    MM(psO[:, es_], Sfb[:, n, es_], qe[0][:, ts_], False, False, [SfbB[n], G0], [pXB])
                    MM(psO[:, es_], st_["sbb"][:, es_], qe[1][:, ts_], False, True, [st_["sbbB"], G1], [pXB])
                s2, s2B = sq.next()
                ACT(s2, psO[:, 0:256], AF.Square, [pXB], [s2B])
                st_["s2"], st_["s2B"] = s2, s2B

            def step_c(n):
                ts_ = slice(n * 128, (n + 1) * 128)
                st_ = stm.pop(n)
                psX, pXB = st_["psX"], st_["pXB"]
                psO = psX[:, 256:512]
                s2, s2B = st_["s2"], st_["s2B"]
                MM(psX[:, 0:128], ones32, s2[:, 0:128], True, False, [onesB, s2B], [pXB])
                MM(psX[:, 0:128], ones32, s2[:, 128:256], False, True, [onesB, s2B], [pXB])
                rs, rsB_ = rsr.next()
                ACT(rs, psX[:, 0:128], AF.Ln, [pXB, onesB], [rsB_], bias=epsc, scale=1.0 / 256)
                ACT(rs, rs, AF.Exp, [rsB_], [rsB_], scale=-0.5)
                go, goB = gout.next()
                for hf in range(2):
                    es_ = slice(hf * 128, (hf + 1) * 128)
                    ot, oB = o32.next()
                    STT("dve", ot, psO[:, es_], gnv[:, hf:hf + 1], rs, ALU.mult, ALU.mult, [pXB, vecB, rsB_], [oB])
                    TTo("dve", go[:, hf, :], ot, gt[:, hf, ts_], ALU.mult, [oB, inB], [goB])
                DMA("sp", catT_v[:, 8 + hb * 2:8 + hb * 2 + 2, ts_], go, [goB], [CTB[8 + hb * 2], CTB[8 + hb * 2 + 1]])

            order_f = [16, 17] + list(range(16))
            prologue(0, order_f)
            for i, c in enumerate(order_f):
                lookahead(0, order_f, i)
                if c < 16:
                    CP("act", Sfb[:, c, :], S[0], [SB_[0]], [SfbB[c]])
                if c != 15:
                    upd(0, c, i == 0)
            order_b = [17, 16] + list(range(15, -1, -1))
            prologue(1, order_b)
            pend_b = []
            pend_c = []
            for i, c in enumerate(order_b):
                lookahead(1, order_b, i)
                if c < 16:
                    step_a(c)
                if pend_b:
                    nb_ = pend_b.pop(0)
                    step_b(nb_)
                    if pend_c:
                        step_c(pend_c.pop(0))
                    pend_c.append(nb_)
                if c < 16:
                    pend_b.append(c)
                if c != 0:
                    upd(1, c, i == 0)
            while pend_b or pend_c:
                if pend_b:
                    nb_ = pend_b.pop(0)
                    step_b(nb_)
                    if pend_c:
                        step_c(pend_c.pop(0))
                    pend_c.append(nb_)
                elif pend_c:
                    step_c(pend_c.pop(0))
        ps_ids[0] = list(range(8))
        drain(bg)
        ar.release()

    def make_resid(gcol, gbcol=None):
        xr = Rot(ar, P, 3, [512], F32, "xres")
        tr = Rot(ar, P, 2, [512], F32, "xrt") if gbcol is not None else None
        cur = {}

        def pre(it):
            c0, cw, m0, (gid, aoff, n) = it
            mi = (c0 + m0) // 128
            t0 = xgrp(gid)[0]
            xt, xB = xr.next()
            DMA("sp", xt[:, 0:n], xT_d[mi * 128:(mi + 1) * 128, t0:t0 + n], [XB[mi][gid]], [xB])
            cur[(mi, gid)] = (xt, xB)

        def epi(mi, gid, ps, pB, m, n):
            t0 = xgrp(gid)[0]
            xt, xB = cur.pop((mi, gid))
            if gbcol is None:
                STT("dve", xt[:, 0:n], ps[:, 0:n], gcol[:, mi:mi + 1], xt[:, 0:n], ALU.mult, ALU.add, [pB, mvB, xB], [xB])
            else:
                tt, tB = tr.next()
                ACT(tt[:, 0:n], ps[:, 0:n], AF.Identity, [pB, mvB], [tB], bias=gbcol[:, mi:mi + 1], scale=gcol[:, mi:mi + 1])
                TTo("dve", xt[:, 0:n], tt[:, 0:n], xt[:, 0:n], ALU.add, [tB, xB], [xB])
            DMA("sp", xT_d[mi * 128:(mi + 1) * 128, t0:t0 + n], xt[:, 0:n], [xB], [XB[mi][gid]])
        return pre, epi

    def phase_outproj(W, l, bias, src_v, srcB):
        ar.mark()
        act = ar.alloc([16, L], BF16)
        aB = [P.buf("oact%d" % g) for g in range(4)]
        for g in range(4):
            DMA("sp", act[:, :, g * 512:(g + 1) * 512], src_v[:, :, g * 512:(g + 1) * 512], list(srcB), [aB[g]])
        pre, epi = make_resid(mv[l][:, 2, :], mv[l][:, 6, :] if bias else None)
        lat = [(g, g * 512, 512) for g in range(4)]
        gemm_fm(W, 0, 16, 0, D, act, lambda g: aB[g], lat, epi, WView(WROT, 16), pre=pre)
        ar.release()

    def phase_mlp(l):
        ar.mark()
        hT = ar.alloc([16, 1024], BF16)
        hB = [P.buf("mh%d" % i) for i in range(2)]
        for tg in range(2):
            norm_phase([(2 * tg, 0), (2 * tg + 1, 512)], hT, lambda g: hB[g % 2],
                       lambda g: (mv[l][:, 3, :], mv[l][:, 4, :]))
            ar.mark()
            uT = ar.alloc([64, 1024], BF16)
            uB = [P.buf("u%d" % i) for i in range(2)]
            sub = [(2 * tg, 0, 512), (2 * tg + 1, 512, 512)]

            rrot = Rot(ar, P, 2, [512], F32, "relu")

            def epi_u(mi, gid, ps, pB, m, n, rrot=rrot, uT=uT, uB=uB):
                s_ = gid % 2
                rt_, rB_ = rrot.next()
                ACT(rt_[:, 0:n], ps[:, 0:n], AF.Relu, [pB], [rB_])
                TTo("dve", uT[:, mi, s_ * 512:(s_ + 1) * 512], rt_[:, 0:n], rt_[:, 0:n], ALU.mult, [rB_], [uB[s_]])
            gemm_fm(w1_d[l], 0, 16, 0, DFF, hT, lambda g: hB[g % 2], sub, epi_u, WView(WROT, 16))
            pre, epi = make_resid(mv[l][:, 5, :])
            gemm_fm(w2_d[l], 0, 64, 0, D, uT, lambda g: uB[g % 2], sub, epi, WView(WROT, 64), bw=128, pre=pre)
            ar.release()
        ar.release()

    UB = Buf("U")
    fTB = Buf("fT")

    def phase_fourier():
        ar.mark()
        hT = ar.alloc([16, L], BF16)
        hB = [P.buf("fh%d" % g) for g in range(4)]
        norm_phase([(g, g * 512) for g in range(4)], hT, lambda g: hB[g], lambda g: (mv[1][:, 0, :], mv[1][:, 1, :]))
        dft = ar.alloc([2, 512], BF16)
        dB = P.buf("dft256")
        DMA("pool", dft, dft256_d.rearrange("(i p) n -> p i n", p=128), (), [dB])
        urot = Rot(ar, P, 3, [8, 512], BF16, "ust")
        for ti in range(16):
            ut, uB_ = urot.next()
            for g in range(8):
                ps, pB = next_ps()
                for i in range(2):
                    MM(ps, hT[:, 2 * g + i, ti * 128:(ti + 1) * 128], dft[:, i, :], i == 0, i == 1, [hB[ti // 4], dB], [pB])
                CP("act" if g % 2 else "dve", ut[:, g, :], ps, [pB], [uB_])
            DMA("sp", U_d[ti * 128:(ti + 1) * 128], ut, [uB_], [UB])
        ar.release()
        ar.mark()
        Uh = ar.alloc([16, 4, 512], BF16)
        UhB = P.buf("Uh")
        trot = Rot(ar, P, 2, [2, 16, 512], BF16, "dftL")
        tcol = ar.alloc([16, 16], BF16)
        tcB = P.buf("tcol")
        DMA("sp", tcol, dftL_d[0, :, 1024:1040].rearrange("(j p) n -> p j n", p=128), (), [tcB])
        o1rot = Rot(ar, P, 3, [512], BF16, "fo1")
        o2rot = Rot(ar, P, 3, [520], BF16, "fo2")
        qrot = Rot(ar, P, 2, [512], F32, "fq")
        for half in range(2):
            for ti in range(16):
                DMA("sp", Uh[:, ti, :, :], U_d[ti * 128:(ti + 1) * 128, half * 4:(half + 1) * 4, :], [UB], [UhB])
            for tb in range(2):
                tab, tB = trot.next()
                for cs in range(2):
                    DMA("sp", tab[:, cs, :, :], dftL_d[cs, :, tb * 512:(tb + 1) * 512].rearrange("(j p) n -> p j n", p=128), (), [tB])
                for gl in range(4):
                    for dh in range(2):
                        dch = (half * 4 + gl) * 2 + dh
                        psP, pPB = next_ps()
                        for ti in range(16):
                            MM(psP, Uh[:, ti, gl, dh * 128:(dh + 1) * 128], tab[:, 0, ti, :], ti == 0, ti == 15, [UhB, tB], [pPB])
                        psQ, pQB = next_ps()
                        for ti in range(16):
                            MM(psQ, Uh[:, ti, gl, 256 + dh * 128:256 + (dh + 1) * 128], tab[:, 1, ti, :], ti == 0, ti == 15,
                               [UhB, tB], [pQB])
                        q32, qB = qrot.next()
                        CP("act", q32, psQ, [pQB], [qB])
                        o1, o1B = o1rot.next()
                        TTo("dve", o1, psP, q32, ALU.subtract, [pPB, qB], [o1B])
                        DMA("sp", fT_d[dch * 128:(dch + 1) * 128, tb * 512:(tb + 1) * 512], o1, [o1B], [fTB])
                        o2, o2B = o2rot.next()
                        TTo("dve", o2[:, 1:513][:, ::-1], psP, q32, ALU.add, [pPB, qB], [o2B])
                        if tb == 0:
                            DMA("sp", fT_d[dch * 128:(dch + 1) * 128, 1537:2048], o2[:, 1:512], [o2B], [fTB])
                        else:
                            ps3, p3B = next_ps()
                            for ti in range(16):
                                MM(ps3[:, 0:1], Uh[:, ti, gl, dh * 128:(dh + 1) * 128], tcol[:, ti, 0:1], ti == 0, ti == 15,
                                   [UhB, tcB], [p3B])
                            CP("act", o2[:, 0:1], ps3[:, 0:1], [p3B], [o2B])
                            DMA("sp", fT_d[dch * 128:(dch + 1) * 128, 1024:1537], o2[:, 0:513], [o2B], [fTB])
        ar.release()

    def phase_out():
        ar.mark()
        rin = Rot(ar, P, 2, [16, 512], F32, "oin")
        rout = Rot(ar, P, 2, [4, D], F32, "oout")
        for g in range(4):
            xg, xB = rin.next()
            DMA("sp", xg, xT_v[:, :, g * 512:(g + 1) * 512], [XB[j][g] for j in range(16)], [xB])
            og, oB = rout.next()
            for i in range(4):
                for jb in range(4):
                    ps, pB = next_ps()
                    for jj in range(4):
                        j = jb * 4 + jj
                        TR(ps[:, jj * 128:(jj + 1) * 128], xg[:, j, i * 128:(i + 1) * 128], ident, [xB, identB], [pB])
                    CP("act" if jb % 2 else "dve", og[:, i, jb * 512:(jb + 1) * 512], ps, [pB], [oB])
            DMA("sp", out_d[g * 512:(g + 1) * 512, :].rearrange("(i p) d -> p i d", p=128), og, [oB], [Buf()])
        ar.release()

    phase_transpose_in()
    if stop_after == "tin":
        return finish()
    phase_inproj()
    if stop_after == "inproj":
        return finish()
    phase_attn()
    if stop_after == "attn":
        return finish()
    phase_gla()
    if stop_after == "gla":
        return finish()
    phase_outproj(wout_d[:, :], 0, False, catT_v, CTB)
    if stop_after == "outproj":
        return finish()
    phase_mlp(0)
    if stop_after == "mlp0":
        return finish()
    phase_fourier()
    if stop_after == "fourier":
        return finish()
    phase_outproj(cw_d[:, :], 1, True, fT_d.rearrange("(j p) t -> p j t", p=128), [fTB])
    if stop_after == "cw":
        return finish()
    phase_mlp(1)
    phase_out()
    return finish()


def _fm(v):
    return np.ascontiguousarray(np.asarray(v, np.float32).reshape(-1, 128).T)


def host_consts():
    c = {}
    c["ident"] = np.eye(128, dtype=np.float32)
    R = np.zeros((128, 128), np.float32)
    for base in (0, 64):
        for i in range(32):
            R[base + i, base + i + 32] = -1.0
            R[base + i + 32, base + i] = 1.0
    c["rt"] = np.ascontiguousarray(R.T)
    kk = np.arange(128)[:, None]
    qq = np.arange(128)[None, :]
    am = np.zeros((128, 2, 128), np.float32)
    am[:, 0, :] = (kk >= qq)
    am[:, 1, :] = (kk <= qq)
    c["amask"] = am
    s = np.arange(128)[:, None]
    cc_ = np.arange(128)[None, :]
    tri = np.zeros((128, 6, 128), np.float32)
    tri[:, 0, :] = (s <= cc_) * (-1.0 / 16)
    tri[:, 1, :] = (s >= cc_) * (-1.0 / 16)
    tri[:, 2, :] = (s > cc_) * (-1.0 / 16)
    tri[:, 3, :] = (s < cc_) * (-1.0 / 16)
    tri[:, 4, :] = (s <= cc_)
    tri[:, 5, :] = (s >= cc_)
    c["tri"] = tri
    t = np.arange(L)
    row = (t // 64).astype(np.float32)
    col = (t % 64).astype(np.float32)
    inv = (10000.0 ** (-np.arange(32, dtype=np.float32) / 32)).astype(np.float32)
    ang_r = row[:, None] * inv
    ang_c = col[:, None] * inv
    cosT = np.zeros((128, L), np.float32)
    sinT = np.zeros((128, L), np.float32)
    for i in range(128):
        a = ang_r if i < 64 else ang_c
        cosT[i] = np.cos(a[:, i % 32])
        sinT[i] = np.sin(a[:, i % 32])
    c["cosT"] = cosT
    c["sinT"] = sinT
    k = np.arange(256)
    a256 = 2 * np.pi * ((k[:, None] * k[None, :]) % 256) / 256
    c["dft256"] = np.concatenate([np.cos(a256), np.sin(a256)], axis=1).astype(np.float32) / 16.0
    tt = np.arange(L, dtype=np.int64)
    aL = 2 * np.pi * ((tt[:, None] * tt[None, :]) % L).astype(np.float64) / L
    sc = 1.0 / np.sqrt(L)
    c["dftL"] = np.ascontiguousarray(np.stack([np.cos(aL) * sc, np.sin(aL) * sc])[:, :, 0:1040]).astype(np.float32).astype(ml_dtypes.bfloat16)
    return c


def make_in_maps(inputs, cores):
    consts = host_consts()
    f = lambda k: np.asarray(inputs[k], np.float32)
    vec = np.zeros((128, NVEC), np.float32)
    vec[:, VC["NM0"]:VC["NM0"] + 16] = _fm(f("norm_mix")[0])
    vec[:, VC["NM1"]:VC["NM1"] + 16] = _fm(f("norm_mix")[1])
    vec[:, VC["NL0"]:VC["NL0"] + 16] = _fm(f("norm_mlp")[0])
    vec[:, VC["NL1"]:VC["NL1"] + 16] = _fm(f("norm_mlp")[1])
    vec[:, VC["CB"]:VC["CB"] + 16] = _fm(f("c_b_out")[0])
    vec[:, VC["AB0"]:VC["AB0"] + 96] = _fm(f("ada_b")[0])
    vec[:, VC["AB1"]:VC["AB1"] + 96] = _fm(f("ada_b")[1])
    vec[:, VC["QN"]] = f("ab_q_norm")[0]
    vec[:, VC["KN"]] = f("ab_k_norm")[0]
    vec[:, VC["GN"]:VC["GN"] + 2] = _fm(f("ab_gla_norm")[0])
    vec[:, VC["SK"]:VC["SK"] + 8] = np.tile(f("ab_sink")[0][None, :], (128, 1))
    gk = np.stack([np.concatenate([f("ab_gk_f")[0], f("ab_gk_f_bias")[0][None, :]], 0),
                   np.concatenate([f("ab_gk_b")[0], f("ab_gk_b_bias")[0][None, :]], 0)]).astype(np.float32)
    shared = dict(vec=vec, gk=gk, gnrow=f("ab_gla_norm")[0][None, :].copy(), ada_w=f("ada_w"), w_in=f("ab_w_in")[0],
                  w_out=f("ab_w_out")[0], c_w=f("c_w_out")[0], w1=f("mlp_w1"), w2=f("mlp_w2"), **consts)
    maps = []
    cfm = _fm(f("c_ctx"))
    for b in cores:
        m = dict(shared)
        m["x"] = np.ascontiguousarray(f("x")[b])
        m["ctx"] = np.ascontiguousarray(f("ctx")[b])
        m["cc"] = np.ascontiguousarray(np.concatenate([_fm(f("c")[b]), cfm], axis=1))
        maps.append(m)
    return maps


def kernel(**inputs):
    nc = build_program()
    maps = make_in_maps(inputs, list(range(8)))
    res = run_bass_kernel_spmd(nc, maps, core_ids=list(range(8)))
    return np.stack([np.asarray(r["out"], np.float32) for r in res.results], axis=0)
```

```python
import numpy as np
import ml_dtypes
import concourse.bass as bass
import concourse.mybir as mybir
from concourse.bass_utils import run_bass_kernel_spmd
from contextlib import ExitStack

F32 = mybir.dt.float32
BF16 = mybir.dt.bfloat16
AF = mybir.ActivationFunctionType
ALU = mybir.AluOpType
AX = mybir.AxisListType

ENGS = ("pe", "act", "dve", "pool", "sp")
NDMASEM = 16
SAME_ENGINE_SYNC = False
EPS = 1e-6

L = 2048
CTX = 256
TT_ = L + CTX
D = 2048
DFF = 8192
INC = 4640


class Buf:
    __slots__ = ("name", "w", "rc", "rd")

    def __init__(self, name="", fence=None):
        self.name = name
        self.w = None
        if fence is None:
            self.rc = {}
            self.rd = []
        else:
            self.rc = dict(fence[0])
            self.rd = list(fence[1])


class Op:
    __slots__ = ("eng", "fn", "deps", "seq", "signal", "semval", "dma", "dslot", "dval", "clock")


class Prog:
    def __init__(self, nc):
        self.nc = nc
        self.stream = {e: [] for e in ENGS}
        self.known = {e: {f: -1 for f in ENGS} for e in ENGS}
        self.known_dma = {e: set() for e in ENGS}
        self.ndma = {e: 0 for e in ENGS}
        self.dma_hist = {e: [] for e in ENGS}
        self.last_c = {}
        self.fence_state = None

    def fence(self):
        rd = []
        for e in ENGS:
            rd.extend(self.dma_hist[e][-NDMASEM:])
        self.fence_state = (dict(self.last_c), rd)

    def buf(self, name=""):
        return Buf(name, self.fence_state)

    def op(self, eng, fn, reads=(), writes=(), dma=False):
        o = Op()
        o.eng = eng
        o.fn = fn
        o.dma = dma
        o.signal = False
        o.semval = 0
        st = self.stream[eng]
        o.seq = len(st)
        deps = []
        for b in reads:
            if b.w is not None:
                deps.append(b.w)
        for b in writes:
            if b.w is not None:
                deps.append(b.w)
            deps.extend(b.rc.values())
            deps.extend(b.rd)
        if dma:
            k = self.ndma[eng]
            self.ndma[eng] = k + 1
            o.dslot = k % NDMASEM
            o.dval = 16 * (k // NDMASEM + 1)
            hist = self.dma_hist[eng]
            if k >= NDMASEM:
                deps.append(hist[k - NDMASEM])
            hist.append(o)
        known = self.known[eng]
        kd = self.known_dma[eng]
        best = {}
        need = []
        for d in deps:
            if d.dma:
                if d in kd:
                    continue
                kd.add(d)
                need.append(d)
            else:
                if d.eng == eng and (eng == "pe" or not SAME_ENGINE_SYNC):
                    continue
                if known[d.eng] >= d.seq:
                    continue
                if d.eng not in best or best[d.eng].seq < d.seq:
                    best[d.eng] = d
        for d in best.values():
            d.signal = True
            need.append(d)
            for f in ENGS:
                if d.clock[f] > known[f]:
                    known[f] = d.clock[f]
            if d.seq > known[d.eng]:
                known[d.eng] = d.seq
        o.deps = need
        clk = dict(known)
        if not dma:
            clk[eng] = o.seq
            self.last_c[eng] = o
        o.clock = clk
        for b in reads:
            if dma:
                b.rd.append(o)
            else:
                b.rc[eng] = o
        for b in writes:
            b.w = o
            b.rc = {}
            b.rd = []
        st.append(o)
        return o

    def emit(self):
        nc = self.nc
        with ExitStack() as es:
            sems = {e: es.enter_context(nc.semaphore("s_" + e)) for e in ENGS}
            dsems = {}
            for e in ENGS:
                if self.ndma[e] > 0:
                    dsems[e] = [es.enter_context(nc.semaphore("d_%s_%d" % (e, i)))
                                for i in range(min(NDMASEM, self.ndma[e]))]
            finals = []
            for e in ENGS:
                for o in reversed(self.stream[e]):
                    if not o.dma:
                        o.signal = True
                        finals.append(o)
                        break
            for e in ENGS:
                c = 0
                for o in self.stream[e]:
                    if o.signal and not o.dma:
                        c += 1
                        o.semval = c
            alld = [o for e in ENGS for o in self.stream[e] if o.dma]

            def emit_waits(eobj, deps):
                for d in deps:
                    if d.dma:
                        eobj.wait_ge(dsems[d.eng][d.dslot], d.dval)
                    else:
                        eobj.wait_ge(sems[d.eng], d.semval)

            def run(ename, eobj):
                for o in self.stream[ename]:
                    emit_waits(eobj, o.deps)
                    ins = o.fn(eobj)
                    if o.dma:
                        ins.then_inc(dsems[ename][o.dslot], 16)
                    elif o.signal:
                        ins.then_inc(sems[ename], 1)
                if ename == "sp":
                    emit_waits(eobj, finals)
                    last = {}
                    for o in alld:
                        last[(o.eng, o.dslot)] = o
                    emit_waits(eobj, list(last.values()))

            with nc.Block() as block:
                @block.tensor
                def _(e):
                    run("pe", e)

                @block.scalar
                def _(e):
                    run("act", e)

                @block.vector
                def _(e):
                    run("dve", e)

                @block.gpsimd
                def _(e):
                    run("pool", e)

                @block.sync
                def _(e):
                    run("sp", e)


class Arena:
    def __init__(self, ap_f32, words, prog):
        self.base = ap_f32
        self.words = words
        self.top = 0
        self.marks = []
        self.prog = prog
        self.peak = 0

    def alloc(self, shape_free, dtype, parts=128):
        n = int(np.prod(shape_free))
        nbytes = n * (2 if dtype == BF16 else 4)
        w = (nbytes + 3) // 4
        w = (w + 7) // 8 * 8
        assert self.top + w <= self.words, ("SBUF arena overflow", self.top, w, self.words)
        v = self.base[0:parts, self.top:self.top + w]
        self.top += w
        self.peak = max(self.peak, self.top)
        if dtype == BF16:
            v = v.bitcast(BF16)[:, 0:n]
        else:
            v = v[:, 0:n]
        if len(shape_free) == 2:
            v = v.rearrange("p (a b) -> p a b", a=shape_free[0])
        elif len(shape_free) == 3:
            v = v.rearrange("p (a b c) -> p a b c", a=shape_free[0], b=shape_free[1])
        return v

    def mark(self):
        self.marks.append(self.top)

    def release(self):
        self.top = self.marks.pop()
        self.prog.fence()


class Rot:
    def __init__(self, ar, P, n, shape, dtype, name="rot", parts=128):
        self.t = [(ar.alloc(shape, dtype, parts), P.buf("%s%d" % (name, i))) for i in range(n)]
        self.i = 0

    def next(self):
        r = self.t[self.i % len(self.t)]
        self.i += 1
        return r


VC = dict(NM0=0, NM1=16, NL0=32, NL1=48, CB=64, AB0=80, AB1=176, QN=272, KN=273, GN=274, SK=276)
NVEC = 284


def build_program(dbg=(), stop_after=None):
    nc = bass.Bass("TRN2", target_bir_lowering=False)

    def din(name, shape, dt=F32):
        return nc.dram_tensor(name, shape, dt, kind="ExternalInput").ap()

    def dscr(name, shape, dt=F32):
        kind = "ExternalOutput" if name in dbg else "Internal"
        return nc.dram_tensor(name, shape, dt, kind=kind).ap()

    x_d = din("x", [L, D])
    ctx_d = din("ctx", [CTX, D])
    cc_d = din("cc", [128, 32])
    vec_d = din("vec", [128, NVEC])
    gk_d = din("gk", [2, 17, 512])
    gnrow_d = din("gnrow", [1, 256])
    adaw_d = din("ada_w", [2, D, 6 * D])
    win_d = din("w_in", [D, INC])
    wout_d = din("w_out", [D, D])
    cw_d = din("c_w", [D, D])
    w1_d = din("w1", [2, D, DFF])
    w2_d = din("w2", [2, DFF, D])
    ident_d = din("ident", [128, 128])
    rt_d = din("rt", [128, 128])
    amask_d = din("amask", [128, 2, 128])
    tri_d = din("tri", [128, 6, 128])
    cos_d = din("cosT", [128, L])
    sin_d = din("sinT", [128, L])
    dft256_d = din("dft256", [256, 512])
    dftL_d = din("dftL", [2, L, 1040], BF16)
    out_d = nc.dram_tensor("out", [L, D], F32, kind="ExternalOutput").ap()

    xT_d = dscr("xT", [D, TT_])
    qT_d = dscr("qT", [8, 128, L], BF16)
    kT_d = dscr("kT", [2, 128, TT_], BF16)
    vA_d = dscr("vA", [TT_, 256], BF16)
    bqT_d = dscr("bqT", [4, 128, L])
    bkT_d = dscr("bkT", [4, 128, TT_])
    bktok_d = dscr("bktok", [TT_, 512])
    bv_d = dscr("bv", [TT_, 1024], BF16)
    gateT_d = dscr("gateT", [1024, L])
    lrT_d = dscr("lrT", [2, 16, TT_])
    catT_d = dscr("catT", [D, L], BF16)
    U_d = dscr("U", [L, 8, 512], BF16)
    fT_d = dscr("fT", [D, L], BF16)
    modT_d = dscr("modT", [2, 128, 192])
    hT_d = dscr("hT", [D, TT_], BF16)

    es = ExitStack()
    AW = 53200
    sb = es.enter_context(nc.sbuf_tensor("arena", [128, AW], F32))
    PS = [es.enter_context(nc.psum_tensor("ps%d" % i, [128, 512], F32)) for i in range(8)]
    P = Prog(nc)
    ar = Arena(sb[:], AW, P)
    PSB = [Buf("psb%d" % i) for i in range(8)]
    psc = [0]

    ps_ids = [list(range(8))]

    def next_ps():
        ids = ps_ids[0]
        i = ids[psc[0] % len(ids)]
        psc[0] += 1
        return PS[i][:], PSB[i]

    def pick_ps(i):
        return PS[i][:], PSB[i]

    def DMA(eng, out, in_, r=(), w=()):
        return P.op(eng, lambda e: e.dma_start(out=out, in_=in_), r, w, dma=True)

    def MM(out, lhsT, rhs, start, stop, r, w):
        return P.op("pe", lambda e: e.matmul(out, lhsT=lhsT, rhs=rhs, start=start, stop=stop), r, w)

    def TR(out, in_, ident, r, w):
        return P.op("pe", lambda e: e.transpose(out, in_, ident), r, w)

    def ACT(out, in_, func, r, w, bias=None, scale=None, accum_out=None):
        kw = {}
        if bias is not None:
            kw["bias"] = bias
        if scale is not None:
            kw["scale"] = scale
        if accum_out is not None:
            kw["accum_out"] = accum_out
        return P.op("act", lambda e: e.activation(out, in_, func, **kw), r, w)

    def TTo(eng, out, in0, in1, op, r, w):
        return P.op(eng, lambda e: e.tensor_tensor(out, in0, in1, op), r, w)

    def STT(eng, out, in0, scalar, in1, op0, op1, r, w):
        return P.op(eng, lambda e: e.scalar_tensor_tensor(out, in0=in0, scalar=scalar, in1=in1, op0=op0, op1=op1), r, w)

    def TS(eng, out, in0, s1, s2, op0, op1, r, w):
        if op1 is None:
            return P.op(eng, lambda e: e.tensor_scalar(out, in0, s1, None, op0=op0), r, w)
        return P.op(eng, lambda e: e.tensor_scalar(out, in0, s1, s2, op0=op0, op1=op1), r, w)

    def CP(eng, out, in_, r, w):
        if eng == "act":
            return P.op("act", lambda e: e.copy(out, in_), r, w)
        return P.op(eng, lambda e: e.tensor_copy(out, in_), r, w)

    def RECIP(out, in_, r, w):
        return P.op("dve", lambda e: e.reciprocal(out, in_), r, w)

    def MEMSET(eng, out, val, w):
        return P.op(eng, lambda e: e.memset(out, val), (), w)

    def pipeline(items, pre, body, la=2):
        n = len(items)
        if pre is not None:
            for i in range(min(la, n)):
                pre(items[i])
        for i in range(n):
            if pre is not None and i + la < n:
                pre(items[i + la])
            body(items[i])

    ident = ar.alloc([128], F32)
    identB = P.buf("ident")
    DMA("sp", ident, ident_d, (), [identB])
    ones32 = ar.alloc([128], F32)
    onesB = P.buf("ones")
    MEMSET("dve", ones32, 1.0, [onesB])
    ones16 = ar.alloc([128], BF16)
    MEMSET("dve", ones16, 1.0, [onesB])
    vec = ar.alloc([NVEC], F32)
    vecB = P.buf("vec")
    DMA("sp", vec, vec_d, (), [vecB])
    mv = [ar.alloc([7, 16], F32) for _ in range(2)]
    cmv = ar.alloc([2, 16], F32)
    mvB = P.buf("mv")
    epsc = ar.alloc([1], F32)
    MEMSET("dve", epsc, EPS, [onesB])

    XB = [[Buf("xT_%d_%d" % (j, g)) for g in range(5)] for j in range(16)]

    def xgrp(g):
        return (g * 512, 512) if g < 4 else (L, CTX)

    xT_v = xT_d.rearrange("(j p) t -> p j t", p=128)

    done = [False]

    def finish():
        P.emit()
        es.close()
        return nc

    cc = ar.alloc([32], F32)
    ccB = P.buf("cc")
    DMA("sp", cc, cc_d, (), [ccB])
    ACT(cc, cc, AF.Silu, [ccB], [ccB])
    sccb = ar.alloc([2, 16], BF16)
    CP("dve", sccb.rearrange("p a b -> p (a b)"), cc, [ccB], [ccB])
    modT = ar.alloc([2, 192], F32)
    mtB = P.buf("modT")

    def mods_gen(l, w=512, nb0=0, nb1=None, fin="ab"):
        wrot = Rot(ar, P, 2, [16, w], BF16, "adaw")
        rrot = Rot(ar, P, 2, [w], F32, "modrow", parts=2)
        nj = w // 128
        for nb in range(nb0, 6 * D // w if nb1 is None else nb1):
            wt, wB = wrot.next()
            DMA("pool", wt, adaw_d[l, :, nb * w:(nb + 1) * w].rearrange("(j p) n -> p j n", p=128), (), [wB])
            ps, pB = next_ps()
            for j in range(16):
                MM(ps[0:2, 0:w], sccb[:, :, j], wt[:, j, :], j == 0, j == 15, [ccB, wB], [pB])
            row, rB = rrot.next()
            CP("act", row[0:2, :], ps[0:2, 0:w], [pB], [rB])
            ps2, p2B = next_ps()
            for jj in range(nj):
                TR(ps2[:, 2 * jj:2 * jj + 2], row[0:2, jj * 128:(jj + 1) * 128], ident[0:2, 0:2], [rB, identB], [p2B])
            ab = vec[:, VC["AB0"] + 96 * l + nb * nj: VC["AB0"] + 96 * l + (nb + 1) * nj]
            for r_ in range(2):
                TTo("dve", modT[:, l, nb * 2 * nj:(nb + 1) * 2 * nj].rearrange("p (j r) -> p j r", r=2)[:, :, r_],
                    ps2[:, 0:2 * nj].rearrange("p (j r) -> p j r", r=2)[:, :, r_], ab, ALU.add, [p2B, vecB], [mtB])
            yield
        m3 = modT[:, l, :].rearrange("p (j r) -> p j r", r=2)
        nm = vec[:, VC["NM0"] + 16 * l: VC["NM0"] + 16 * (l + 1)]
        nl = vec[:, VC["NL0"] + 16 * l: VC["NL0"] + 16 * (l + 1)]
        if "a" in fin:
            CP("dve", mv[l][:, 0, :], m3[:, 0:16, 0], [mtB], [mvB])
            STT("dve", mv[l][:, 1, :], m3[:, 16:32, 0], 1.0, nm, ALU.add, ALU.mult, [mtB, vecB], [mvB])
            if l == 0:
                CP("dve", cmv[:, 0, :], m3[:, 0:16, 1], [mtB], [mvB])
                STT("dve", cmv[:, 1, :], m3[:, 16:32, 1], 1.0, vec[:, VC["NM0"]:VC["NM0"] + 16], ALU.add, ALU.mult,
                    [mtB, vecB], [mvB])
        if "b" in fin:
            CP("dve", mv[l][:, 2, :], m3[:, 32:48, 0], [mtB], [mvB])
            CP("dve", mv[l][:, 3, :], m3[:, 48:64, 0], [mtB], [mvB])
            STT("dve", mv[l][:, 4, :], m3[:, 64:80, 0], 1.0, nl, ALU.add, ALU.mult, [mtB, vecB], [mvB])
            CP("dve", mv[l][:, 5, :], m3[:, 80:96, 0], [mtB], [mvB])
            TTo("dve", mv[l][:, 6, :], m3[:, 32:48, 0], vec[:, VC["CB"]:VC["CB"] + 16], ALU.mult, [mtB, vecB], [mvB])
        yield

    def drain(gen):
        if gen is not None:
            for _ in gen:
                pass

    def phase_transpose_in():
        ar.mark()
        bg = mods_gen(0, 512, 0, 8, "a")
        rin = Rot(ar, P, 2, [4, D], F32, "xin")
        rout = Rot(ar, P, 2, [16, 512], F32, "xstg")
        cur = {}

        def pre(g):
            xin, xB = rin.next()
            if g < 4:
                DMA("sp", xin, x_d[g * 512:(g + 1) * 512, :].rearrange("(i p) d -> p i d", p=128), (), [xB])
            else:
                DMA("sp", xin[:, 0:2, :], ctx_d.rearrange("(i p) d -> p i d", p=128), (), [xB])
            cur[g] = (xin, xB)

        def body(g):
            xin, xB = cur.pop(g)
            t0, n = xgrp(g)
            nt = n // 128
            stg, sB = rout.next()
            for j in range(16):
                ps, pB = next_ps()
                for i in range(nt):
                    TR(ps[:, i * 128:(i + 1) * 128], xin[:, i, j * 128:(j + 1) * 128], ident, [xB, identB], [pB])
                CP("act" if j % 2 else "dve", stg[:, j, 0:n], ps[:, 0:n], [pB], [sB])
                if j % 3 == 0:
                    next(bg, None)
            DMA("sp", xT_v[:, :, t0:t0 + n], stg[:, :, 0:n], [sB], [XB[j][g] for j in range(16)])

        pipeline(list(range(5)), pre, body, la=1)
        drain(bg)
        ar.release()

    def red(out, in_, r, w):
        return P.op("dve", lambda e: e.tensor_reduce(out, in_, AX.X, ALU.add), r, w)

    def norm_phase(groups, hT, hB_of, scal_of):
        ar.mark()
        rx = Rot(ar, P, 2, [16, 512], F32, "nx")
        sqr = Rot(ar, P, 2, [4, 512], F32, "nsq")
        ssum = ar.alloc([4, 512], F32)
        ssB = P.buf("nss")
        rstd = ar.alloc([512], F32)
        rsB = P.buf("nrs")
        rtmp = Rot(ar, P, 3, [512], F32, "ntmp")
        cur = {}

        def pre(it):
            g, off = it
            t0, n = xgrp(g)
            xg, xB = rx.next()
            DMA("sp", xg[:, :, 0:n], xT_v[:, :, t0:t0 + n], [XB[j][g] for j in range(16)], [xB])
            cur[g] = (xg, xB)

        def body(it):
            g, off = it
            t0, n = xgrp(g)
            xg, xB = cur.pop(g)
            sh, gsc = scal_of(g)
            ps, pB = next_ps()
            for q_ in range(4):
                sq, sqB = sqr.next()
                ACT(sq[:, :, 0:n], xg[:, 4 * q_:4 * q_ + 4, 0:n], AF.Square, [xB], [sqB])
                red(ssum[:, q_, 0:n], sq[:, :, 0:n].rearrange("p j t -> p t j"), [sqB], [ssB])
                MM(ps[:, 0:n], ones32, ssum[:, q_, 0:n], q_ == 0, q_ == 3, [onesB, ssB], [pB])
            ACT(rstd[:, 0:n], ps[:, 0:n], AF.Sqrt, [pB, onesB], [rsB], bias=epsc, scale=1.0 / D)
            RECIP(rstd[:, 0:n], rstd[:, 0:n], [rsB], [rsB])
            for j in range(16):
                tmp, tB = rtmp.next()
                STT("dve", tmp[:, 0:n], xg[:, j, 0:n], gsc[:, j:j + 1], rstd[:, 0:n], ALU.mult, ALU.mult,
                    [xB, mvB, rsB], [tB])
                ACT(hT[:, j, off:off + n], tmp[:, 0:n], AF.Identity, [tB, mvB], [hB_of(g)], bias=sh[:, j:j + 1])

        pipeline(groups, pre, body, la=1)
        ar.release()

    WROT = Rot(ar, P, 2, [8192], BF16, "W")

    class WView:
        def __init__(self, rot, a):
            self.rot, self.a = rot, a

        def next(self):
            t, B_ = self.rot.next()
            return t.rearrange("p (a b) -> p a b", a=self.a), B_

    def run(gen):
        for _ in gen:
            pass

    def chain(*gens):
        for g_ in gens:
            for _ in g_:
                yield

    def interleave(ga, na, gb_, nb):
        da = db = False
        while not (da and db):
            for _ in range(na):
                if not da:
                    try:
                        next(ga)
                    except StopIteration:
                        da = True
            for _ in range(nb):
                if not db:
                    try:
                        next(gb_)
                    except StopIteration:
                        db = True

    def gemm_fm_gen(W, row0, KC, col0, ncols, act, actB_of, groups, epi, wrot, bw=512, mch=128, pre=None, la=2, psfn=None):
        items = []
        for c0 in range(0, ncols, bw):
            cw = min(bw, ncols - c0)
            for m0 in range(0, cw, mch):
                for gi in groups:
                    items.append((c0, cw, m0, gi))
        state = {"c0": None}
        nit = len(items)
        if pre is not None:
            for i in range(min(la, nit)):
                pre(items[i])
        for i, it in enumerate(items):
            if pre is not None and i + la < nit:
                pre(items[i + la])
            c0, cw, m0, (gid, aoff, n) = it
            if state["c0"] != c0:
                wt, wB = wrot.next()
                DMA("pool", wt[:, 0:KC, 0:cw],
                    W[row0:row0 + KC * 128, col0 + c0:col0 + c0 + cw].rearrange("(j p) n -> p j n", p=128), (), [wB])
                state["c0"] = c0
                state["w"] = (wt, wB)
            wt, wB = state["w"]
            m = min(mch, cw - m0)
            ps, pB = (psfn or next_ps)()
            for j in range(KC):
                MM(ps[0:m, 0:n], wt[:, j, m0:m0 + m], act[:, j, aoff:aoff + n], j == 0, j == KC - 1,
                   [wB, actB_of(gid)], [pB])
            epi((c0 + m0) // mch, gid, ps, pB, m, n)
            yield

    def gemm_fm(*a_, **k_):
        run(gemm_fm_gen(*a_, **k_))

    def gemm_tm_gen(W, row0, KC, col0, ncols, act, actB_of, tiles, epi, wrot):
        for c0 in range(0, ncols, 512):
            cw = min(512, ncols - c0)
            wt, wB = wrot.next()
            DMA("pool", wt[:, 0:KC, 0:cw],
                W[row0:row0 + KC * 128, col0 + c0:col0 + c0 + cw].rearrange("(j p) n -> p j n", p=128), (), [wB])
            for (ti, aoff, gid) in tiles:
                ps, pB = next_ps()
                for j in range(KC):
                    MM(ps[:, 0:cw], act[:, j, aoff:aoff + 128], wt[:, j, 0:cw], j == 0, j == KC - 1,
                       [wB, actB_of(gid)], [pB])
                epi(c0, cw, ti, ps, pB)
                yield

    qTB = Buf("qT")
    kTB = Buf("kT")
    vAB = Buf("vA")
    bqTB = Buf("bqT")
    bkTB = Buf("bkT")
    bktokB = Buf("bktok")
    bvB = Buf("bv")
    gateTB = Buf("gateT")
    lrTB = Buf("lrT")

    CTB = [Buf("catT%d" % i) for i in range(16)]
    catT_v = catT_d.rearrange("(j p) t -> p j t", p=128)

    def phase_inproj():
        ar.mark()
        hT = ar.alloc([16, TT_], BF16)
        hB = [P.buf("hT%d" % g) for g in range(5)]
        norm_phase([(g, xgrp(g)[0]) for g in range(5)], hT, lambda g: hB[g],
                   lambda g: (mv[0][:, 0, :], mv[0][:, 1, :]) if g < 4 else (cmv[:, 0, :], cmv[:, 1, :]))
        if "hT" in dbg:
            DMA("sp", hT_d.rearrange("(j p) t -> p j t", p=128), hT, hB, [Buf()])
        wA = WView(WROT, 16)
        wB_ = WView(Rot(ar, P, 2, [8192], BF16, "winB"), 16)
        o16 = Rot(ar, P, 3, [512], BF16, "o16")
        o32 = Rot(ar, P, 3, [512], F32, "o32")
        lat = [(g, g * 512, 512) for g in range(4)]
        allg = lat + [(4, L, CTX)]
        hBf = lambda g: hB[g]
        tiles_all = [(i, i * 128, (i // 4) if i < 16 else 4) for i in range(18)]
        ar.mark()
        cosT = ar.alloc([L], F32)
        sinT = ar.alloc([L], F32)
        rt32 = ar.alloc([128], F32)
        cB = P.buf("cossin")
        DMA("sp", cosT, cos_d, (), [cB])
        DMA("sp", sinT, sin_d, (), [cB])
        DMA("sp", rt32, rt_d, (), [cB])
        r_sq = Rot(ar, P, 3, [512], F32, "qsq")
        r_rs = Rot(ar, P, 2, [512], F32, "qrs")
        r_qn = Rot(ar, P, 3, [512], F32, "qqn")
        r_t1 = Rot(ar, P, 2, [512], F32, "qt1")
        r_t2 = Rot(ar, P, 2, [512], F32, "qt2")
        deferred = []
        hctr = [0]
        ectr = [0]

        def ps_heavy():
            hctr[0] += 1
            return pick_ps(hctr[0] % 2)

        def ps_epi():
            ectr[0] += 1
            return pick_ps(2 + ectr[0] % 2)

        def epi_qk(dst, dstB, gvec):
            def epi(mi, gid, ps, pB, m, n):
                t0 = xgrp(gid)[0]
                sq, sB = r_sq.next()
                ACT(sq[:, 0:n], ps[:, 0:n], AF.Square, [pB], [sB])

                def stage2():
                    ps2, p2B = ps_epi()
                    MM(ps2[:, 0:n], ones32, sq[:, 0:n], True, True, [onesB, sB], [p2B])
                    rs, rB = r_rs.next()
                    ACT(rs[:, 0:n], ps2[:, 0:n], AF.Sqrt, [p2B, onesB], [rB], bias=epsc, scale=1.0 / 128)
                    RECIP(rs[:, 0:n], rs[:, 0:n], [rB], [rB])
                    qn, qB = r_qn.next()
                    STT("dve", qn[:, 0:n], ps[:, 0:n], gvec, rs[:, 0:n], ALU.mult, ALU.mult, [pB, vecB, rB], [qB])
                    if gid < 4:
                        def stage3():
                            ot, oB = o16.next()
                            ps3, p3B = ps_epi()
                            MM(ps3[:, 0:n], rt32, qn[:, 0:n], True, True, [cB, qB], [p3B])
                            t1, t1B = r_t1.next()
                            TTo("dve", t1[:, 0:n], qn[:, 0:n], cosT[:, t0:t0 + n], ALU.mult, [qB, cB], [t1B])
                            t2, t2B = r_t2.next()
                            TTo("dve", t2[:, 0:n], ps3[:, 0:n], sinT[:, t0:t0 + n], ALU.mult, [p3B, cB], [t2B])
                            TTo("dve", ot[:, 0:n], t1[:, 0:n], t2[:, 0:n], ALU.add, [t1B, t2B], [oB])
                            DMA("sp", dst[mi, :, t0:t0 + n], ot[:, 0:n], [oB], [dstB])
                        deferred.append(stage3)
                    else:
                        ot, oB = o16.next()
                        CP("dve", ot[:, 0:n], qn[:, 0:n], [qB], [oB])
                        DMA("sp", dst[mi, :, t0:t0 + n], ot[:, 0:n], [oB], [dstB])
                deferred.append(stage2)
            return epi

        def epi_v(dst, dstB, rot):
            def epi(c0, cw, ti, ps, pB):
                ot, oB = rot.next()
                CP("act" if ti % 2 else "dve", ot[:, 0:cw], ps[:, 0:cw], [pB], [oB])
                DMA("sp", dst[ti * 128:(ti + 1) * 128, c0:c0 + cw], ot[:, 0:cw], [oB], [dstB])
            return epi

        def epi_bq(mi, gid, ps, pB, m, n):
            t0 = xgrp(gid)[0]
            ot, oB = o32.next()
            ACT(ot[:, 0:n], ps[:, 0:n], AF.Identity, [pB], [oB], scale=float(128 ** -0.5))
            DMA("sp", bqT_d[mi, :, t0:t0 + n], ot[:, 0:n], [oB], [bqTB])

        def epi_bk(mi, gid, ps, pB, m, n):
            t0 = xgrp(gid)[0]
            ot, oB = o32.next()
            CP("dve", ot[:, 0:n], ps[:, 0:n], [pB], [oB])
            DMA("sp", bkT_d[mi, :, t0:t0 + n], ot[:, 0:n], [oB], [bkTB])

        def epi_gate(mi, gid, ps, pB, m, n):
            t0 = xgrp(gid)[0]
            ot, oB = o32.next()
            ACT(ot[:, 0:n], ps[:, 0:n], AF.Silu, [pB], [oB])
            DMA("sp", gateT_d[mi * 128:(mi + 1) * 128, t0:t0 + n], ot[:, 0:n], [oB], [gateTB])

        def epi_lr(mi, gid, ps, pB, m, n):
            t0 = xgrp(gid)[0]
            ot, oB = o32.next()
            CP("dve", ot[0:16, 0:n], ps[0:16, 0:n], [pB], [oB])
            DMA("sp", lrT_d[mi, :, t0:t0 + n], ot[0:16, 0:n], [oB], [lrTB])

        genA = chain(
            gemm_fm_gen(win_d, 0, 16, 0, 1024, hT, hBf, lat, epi_qk(qT_d, qTB, vec[:, VC["QN"]:VC["QN"] + 1]), wA,
                        psfn=ps_heavy),
            gemm_fm_gen(win_d, 0, 16, 1024, 256, hT, hBf, allg, epi_qk(kT_d, kTB, vec[:, VC["KN"]:VC["KN"] + 1]), wA,
                        psfn=ps_heavy),
            gemm_tm_gen(win_d, 0, 16, 1280, 256, hT, hBf, tiles_all, epi_v(vA_d, vAB, o16), wA))
        genB = chain(
            gemm_fm_gen(win_d, 0, 16, 1536, 512, hT, hBf, lat, epi_bq, wB_),
            gemm_fm_gen(win_d, 0, 16, 2048, 512, hT, hBf, allg, epi_bk, wB_),
            gemm_fm_gen(win_d, 0, 16, 4608, 32, hT, hBf, allg, epi_lr, wB_, bw=32, mch=16),
            gemm_tm_gen(win_d, 0, 16, 2560, 1024, hT, hBf, tiles_all, epi_v(bv_d, bvB, o16), wB_),
            gemm_tm_gen(win_d, 0, 16, 2048, 512, hT, hBf, tiles_all, epi_v(bktok_d, bktokB, o32), wB_),
            gemm_fm_gen(win_d, 0, 16, 3584, 1024, hT, hBf, lat, epi_gate, wB_))
        ps_ids[0] = [4, 5, 6, 7]

        def run_deferred():
            d_ = deferred[:]
            del deferred[:]
            for f_ in d_:
                f_()
        doneA = False
        while not doneA or deferred:
            if not doneA:
                try:
                    next(genA)
                except StopIteration:
                    doneA = True
            next(genB, None)
            run_deferred()
            next(genB, None)
            run_deferred()
        ps_ids[0] = list(range(8))
        run(genB)
        ar.release()
        ar.release()

    def phase_attn():
        ar.mark()
        bg2 = mods_gen(0, 512, 8, 24, "b")
        ps_ids[0] = [7]
        for _ in attn_gen():
            next(bg2, None)
        drain(bg2)
        ps_ids[0] = list(range(8))
        ar.release()

    def attn_gen():
        am = ar.alloc([2, 128], BF16)
        amB = P.buf("amask")
        DMA("pool", am, amask_d, (), [amB])
        esink = ar.alloc([8], F32)
        esB = P.buf("esink")
        ACT(esink, vec[:, VC["SK"]:VC["SK"] + 8], AF.Exp, [vecB], [esB])
        kT = ar.alloc([TT_], BF16)
        vS = ar.alloc([18, 128], BF16)
        qS = ar.alloc([4, L], BF16)
        kvB = P.buf("kv")
        prot = Rot(ar, P, 6, [4, 128], BF16, "pT")
        orot = Rot(ar, P, 3, [4, 128], BF16, "aout")
        den = ar.alloc([4, 128], F32)
        denB = P.buf("den")
        scale = float(128 ** -0.5)
        sctr = [0]
        for kv in range(2):
            DMA("sp", kT, kT_d[kv], [kTB], [kvB])
            DMA("sp", vS, vA_d[:, kv * 128:(kv + 1) * 128].rearrange("(c p) d -> p c d", p=128), [vAB], [kvB])
            DMA("sp", qS, qT_d[kv * 4:(kv + 1) * 4].rearrange("h p t -> p h t"), [qTB], [kvB])
            for n in range(16):
                blocks = []
                if n > 0:
                    blocks.append((n - 1, 0))
                blocks.append((n, None))
                if n < 15:
                    blocks.append((n + 1, 1))
                blocks += [(16, None), (17, None)]
                psO, pOB = pick_ps(0)
                psD, pDB = pick_ps(1)
                nb_ = len(blocks)
                sps = []
                for bi, (kc, mk) in enumerate(blocks):
                    psS, pSB = pick_ps(2 + bi)
                    MM(psS.rearrange("p (h q) -> p h q", h=4), kT[:, kc * 128:(kc + 1) * 128],
                       qS[:, :, n * 128:(n + 1) * 128], True, True, [kvB], [pSB])
                    sps.append((psS, pSB))
                for bi, (kc, mk) in enumerate(blocks):
                    psS, pSB = sps[bi]
                    pT, pB_ = prot.next()
                    ACT(pT, psS.rearrange("p (h q) -> p h q", h=4), AF.Exp, [pSB], [pB_], scale=scale)
                    if mk is not None:
                        TTo("dve", pT, pT, am[:, mk, :].unsqueeze(1).to_broadcast([128, 4, 128]), ALU.mult,
                            [pB_, amB], [pB_])
                    MM(psO.rearrange("p (h q) -> p h q", h=4), vS[:, kc, :], pT, bi == 0, bi == nb_ - 1, [kvB, pB_], [pOB])
                    MM(psD.rearrange("p (h q) -> p h q", h=4), ones16, pT, bi == 0, bi == nb_ - 1, [onesB, pB_], [pDB])
                TTo("dve", den, psD.rearrange("p (h q) -> p h q", h=4),
                    esink[:, kv * 4:(kv + 1) * 4].unsqueeze(2).to_broadcast([128, 4, 128]), ALU.add, [pDB, esB], [denB])
                RECIP(den, den, [denB], [denB])
                ot, oB = orot.next()
                TTo("dve", ot, psO.rearrange("p (h q) -> p h q", h=4), den, ALU.mult, [pOB, denB], [oB])
                DMA("sp", catT_v[:, kv * 4:(kv + 1) * 4, n * 128:(n + 1) * 128], ot, [oB],
                    [CTB[kv * 4 + h] for h in range(4)])
                yield

    def phase_gla():
        ar.mark()
        bg = mods_gen(1, 512)
        gctr = [0]
        gout = Rot(ar, P, 3, [2, 128], BF16, "gout")
        tri = ar.alloc([6, 128], F32)
        triB = P.buf("tri")
        DMA("sp", tri, tri_d, (), [triB])
        gk = ar.alloc([2, 512], F32, parts=17)
        DMA("sp", gk, gk_d.rearrange("r k n -> k r n"), (), [triB])
        lr = ar.alloc([2, TT_], F32, parts=17)
        lrB = P.buf("lr")
        MEMSET("dve", lr[0:17, :, :], 1.0, [lrB])
        DMA("sp", lr[0:16, :, :], lrT_d.rearrange("r k t -> k r t"), [lrTB], [lrB])
        gnv = vec[:, VC["GN"]:VC["GN"] + 2]
        bq = ar.alloc([L], F32)
        bk = ar.alloc([TT_], F32)
        bkt = ar.alloc([18, 128], F32)
        vS = ar.alloc([18, 256], BF16)
        gt = ar.alloc([2, L], F32)
        inB = P.buf("glain")
        qe = [ar.alloc([L], BF16) for _ in range(2)]
        ke = [ar.alloc([L], BF16) for _ in range(2)]
        kd = [ar.alloc([18, 128], BF16) for _ in range(2)]
        dec = [ar.alloc([18], F32) for _ in range(2)]
        gB = [[P.buf("gla_g%d_%d" % (d, c)) for c in range(18)] for d in range(2)]
        S = [ar.alloc([256], F32) for _ in range(2)]
        SB_ = [P.buf("S%d" % d) for d in range(2)]
        Sfb = ar.alloc([16, 256], BF16)
        SfbB = [P.buf("Sfb%d" % i) for i in range(16)]
        Sbb = Rot(ar, P, 3, [256], BF16, "Sbb")
        r_e1 = Rot(ar, P, 2, [128], F32, "ge1")
        r_sp = Rot(ar, P, 3, [128], F32, "gsp")
        r_eb = Rot(ar, P, 2, [128], F32, "geb")
        r_enb = Rot(ar, P, 2, [128], F32, "genb")
        r_eD = Rot(ar, P, 2, [128], F32, "geD")
        r_a1 = Rot(ar, P, 2, [128], F32, "ga1")
        r_a2 = Rot(ar, P, 2, [128], F32, "ga2")
        at16 = Rot(ar, P, 3, [128], BF16, "AT")
        sq = Rot(ar, P, 3, [256], F32, "gsq")
        rsr = Rot(ar, P, 2, [128], F32, "grs")
        o32 = Rot(ar, P, 3, [128], F32, "go32")
        zctr = [0]
        xctr = [0]
        ps_ids[0] = [6, 7]
        LA = 3
        for hb in range(4):
            DMA("sp", bq, bqT_d[hb], [bqTB], [inB])
            DMA("sp", bk, bkT_d[hb], [bkTB], [inB])
            DMA("sp", bkt, bktok_d[:, hb * 128:(hb + 1) * 128].rearrange("(c p) d -> p c d", p=128), [bktokB], [inB])
            DMA("sp", vS, bv_d[:, hb * 256:(hb + 1) * 256].rearrange("(c p) e -> p c e", p=128), [bvB], [inB])
            DMA("sp", gt, gateT_d[hb * 256:(hb + 1) * 256, :].rearrange("(a p) t -> p a t", p=128), [gateTB], [inB])
            spm = {}

            def gates_a(d_, c):
                gctr[0] += 1
                if gctr[0] % 5 == 0:
                    next(bg, None)
                ts_ = slice(c * 128, (c + 1) * 128)
                psz, pzB = pick_ps(zctr[0] % 2)
                zctr[0] += 1
                MM(psz[:, 0:128], lr[0:17, d_, ts_], gk[0:17, d_, hb * 128:(hb + 1) * 128], True, True, [lrB, triB], [pzB])
                e1, e1B = r_e1.next()
                ACT(e1, psz[:, 0:128], AF.Exp, [pzB], [e1B], scale=-1.0)
                sp_, spB = r_sp.next()
                ACT(sp_, e1, AF.Ln, [e1B, onesB], [spB], bias=ones32[:, 0:1])
                spm[(d_, c)] = (sp_, spB, psz, pzB)

            def gates_b(d_, c):
                ts_ = slice(c * 128, (c + 1) * 128)
                G = gB[d_][c]
                sp_, spB, psz, pzB = spm.pop((d_, c))
                MM(psz[:, 128:256], sp_, tri[:, d_, :], True, True, [spB, triB], [pzB])
                MM(psz[:, 256:384], tri[:, 2 + d_, :], sp_, True, True, [spB, triB], [pzB])
                eb, ebB = r_eb.next()
                ACT(eb, psz[:, 128:256], AF.Exp, [pzB], [ebB])
                col = 127 if d_ == 0 else 0
                CP("dve", dec[d_][:, c:c + 1], eb[:, col:col + 1], [ebB], [G])
                if c < 16:
                    TTo("dve", qe[d_][:, ts_], bq[:, ts_], eb, ALU.mult, [inB, ebB], [G])
                    enb, enB = r_enb.next()
                    ACT(enb, psz[:, 128:256], AF.Exp, [pzB], [enB], scale=-1.0)
                    TTo("dve", ke[d_][:, ts_], bk[:, ts_], enb, ALU.mult, [inB, enB], [G])
                eD, eDB = r_eD.next()
                ACT(eD, psz[:, 256:384], AF.Exp, [pzB], [eDB])
                TTo("dve", kd[d_][:, c, :], bkt[:, c, :], eD, ALU.mult, [inB, eDB], [G])

            def upd(d_, c, first):
                ps, pB = pick_ps(5)
                MM(ps[:, 0:256], kd[d_][:, c, :], vS[:, c, :], True, True, [gB[d_][c], inB], [pB])
                if first:
                    CP("dve", S[d_], ps[:, 0:256], [pB], [SB_[d_]])
                else:
                    STT("dve", S[d_], S[d_], dec[d_][:, c:c + 1], ps[:, 0:256], ALU.mult, ALU.add,
                        [SB_[d_], gB[d_][c], pB], [SB_[d_]])

            def prologue(d_, order):
                for k in range(LA + 1):
                    gates_a(d_, order[k])
                    if k > 0:
                        gates_b(d_, order[k - 1])

            def lookahead(d_, order, i):
                if i + LA + 1 < len(order):
                    gates_a(d_, order[i + LA + 1])
                if i + LA < len(order):
                    gates_b(d_, order[i + LA])

            stm = {}

            def step_a(n):
                ts_ = slice(n * 128, (n + 1) * 128)
                G0, G1 = gB[0][n], gB[1][n]
                sbb, sbbB = Sbb.next()
                CP("act", sbb, S[1], [SB_[1]], [sbbB])
                psX, pXB = pick_ps(2 + xctr[0] % 3)
                xctr[0] += 1
                MM(psX[:, 0:128], ke[0][:, ts_], qe[0][:, ts_], True, True, [G0], [pXB])
                MM(psX[:, 128:256], ke[1][:, ts_], qe[1][:, ts_], True, True, [G1], [pXB])
                a1, a1B = r_a1.next()
                TTo("dve", a1, psX[:, 0:128], tri[:, 4, :], ALU.mult, [pXB, triB], [a1B])
                a2, a2B = r_a2.next()
                TTo("dve", a2, psX[:, 128:256], tri[:, 5, :], ALU.mult, [pXB, triB], [a2B])
                AT, atB = at16.next()
                TTo("dve", AT, a1, a2, ALU.add, [a1B, a2B], [atB])
                stm[n] = dict(psX=psX, pXB=pXB, AT=AT, atB=atB, sbb=sbb, sbbB=sbbB)

            def step_b(n):
                ts_ = slice(n * 128, (n + 1) * 128)
                G0, G1 = gB[0][n], gB[1][n]
                st_ = stm[n]
                psX, pXB = st_["psX"], st_["pXB"]
                psO = psX[:, 256:512]
                for hf in range(2):
                    es_ = slice(hf * 128, (hf + 1) * 128)
                    MM(psO[:, es_], vS[:, n, es_], st_["AT"], True, False, [inB, st_["atB"]], [pXB])
                    MM(psO[:, es_], Sfb[:, n, es_], qe[0][:, ts_], False, False, [SfbB[n], G0], [pXB])
                    MM(psO[:, es_], st_["sbb"][:, es_], qe[1][:, ts_], False, True, [st_["sbbB"], G1], [pXB])
                s2, s2B = sq.next()
                ACT(s2, psO[:, 0:256], AF.Square, [pXB], [s2B])
                st_["s2"], st_["s2B"] = s2, s2B

            def step_c(n):
                ts_ = slice(n * 128, (n + 1) * 128)
                st_ = stm.pop(n)
                psX, pXB = st_["psX"], st_["pXB"]
                psO = psX[:, 256:512]
                s2, s2B = st_["s2"], st_["s2B"]
                MM(psX[:, 0:128], ones32, s2[:, 0:128], True, False, [onesB, s2B], [pXB])
                MM(psX[:, 0:128], ones32, s2[:, 128:256], False, True, [onesB, s2B], [pXB])
                rs, rsB_ = rsr.next()
                ACT(rs, psX[:, 0:128], AF.Ln, [pXB, onesB], [rsB_], bias=epsc, scale=1.0 / 256)
                ACT(rs, rs, AF.Exp, [rsB_], [rsB_], scale=-0.5)
                go, goB = gout.next()
                for hf in range(2):
                    es_ = slice(hf * 128, (hf + 1) * 128)
                    ot, oB = o32.next()
                    STT("dve", ot, psO[:, es_], gnv[:, hf:hf + 1], rs, ALU.mult, ALU.mult, [pXB, vecB, rsB_], [oB])
                    TTo("dve", go[:, hf, :], ot, gt[:, hf, ts_], ALU.mult, [oB, inB], [goB])
                DMA("sp", catT_v[:, 8 + hb * 2:8 + hb * 2 + 2, ts_], go, [goB], [CTB[8 + hb * 2], CTB[8 + hb * 2 + 1]])

            order_f = [16, 17] + list(range(16))
            prologue(0, order_f)
            for i, c in enumerate(order_f):
                lookahead(0, order_f, i)
                if c < 16:
                    CP("act", Sfb[:, c, :], S[0], [SB_[0]], [SfbB[c]])
                if c != 15:
                    upd(0, c, i == 0)
            order_b = [17, 16] + list(range(15, -1, -1))
            prologue(1, order_b)
            pend_b = []
            pend_c = []
            for i, c in enumerate(order_b):
                lookahead(1, order_b, i)
                if c < 16:
                    step_a(c)
                if pend_b:
                    nb_ = pend_b.pop(0)
                    step_b(nb_)
                    if pend_c:
                        step_c(pend_c.pop(0))
                    pend_c.append(nb_)
                if c < 16:
                    pend_b.append(c)
                if c != 0:
                    upd(1, c, i == 0)
            while pend_b or pend_c:
                if pend_b:
                    nb_ = pend_b.pop(0)
                    step_b(nb_)
                    if pend_c:
                        step_c(pend_c.pop(0))
                    pend_c.append(nb_)
                elif pend_c:
                    step_c(pend_c.pop(0))
        ps_ids[0] = list(range(8))
        drain(bg)
        ar.release()

    def make_resid(gcol, gbcol=None):
        xr = Rot(ar, P, 3, [512], F32, "xres")
        tr = Rot(ar, P, 2, [512], F32, "xrt") if gbcol is not None else None
        cur = {}

        def pre(it):
            c0, cw, m0, (gid, aoff, n) = it
            mi = (c0 + m0) // 128
            t0 = xgrp(gid)[0]
            xt, xB = xr.next()
            DMA("sp", xt[:, 0:n], xT_d[mi * 128:(mi + 1) * 128, t0:t0 + n], [XB[mi][gid]], [xB])
            cur[(mi, gid)] = (xt, xB)

        def epi(mi, gid, ps, pB, m, n):
            t0 = xgrp(gid)[0]
            xt, xB = cur.pop((mi, gid))
            if gbcol is None:
                STT("dve", xt[:, 0:n], ps[:, 0:n], gcol[:, mi:mi + 1], xt[:, 0:n], ALU.mult, ALU.add, [pB, mvB, xB], [xB])
            else:
                tt, tB = tr.next()
                ACT(tt[:, 0:n], ps[:, 0:n], AF.Identity, [pB, mvB], [tB], bias=gbcol[:, mi:mi + 1], scale=gcol[:, mi:mi + 1])
                TTo("dve", xt[:, 0:n], tt[:, 0:n], xt[:, 0:n], ALU.add, [tB, xB], [xB])
            DMA("sp", xT_d[mi * 128:(mi + 1) * 128, t0:t0 + n], xt[:, 0:n], [xB], [XB[mi][gid]])
        return pre, epi

    def phase_outproj(W, l, bias, src_v, srcB):
        ar.mark()
        act = ar.alloc([16, L], BF16)
        aB = [P.buf("oact%d" % g) for g in range(4)]
        for g in range(4):
            DMA("sp", act[:, :, g * 512:(g + 1) * 512], src_v[:, :, g * 512:(g + 1) * 512], list(srcB), [aB[g]])
        pre, epi = make_resid(mv[l][:, 2, :], mv[l][:, 6, :] if bias else None)
        lat = [(g, g * 512, 512) for g in range(4)]
        gemm_fm(W, 0, 16, 0, D, act, lambda g: aB[g], lat, epi, WView(WROT, 16), pre=pre)
        ar.release()

    def phase_mlp(l):
        ar.mark()
        hT = ar.alloc([16, 1024], BF16)
        hB = [P.buf("mh%d" % i) for i in range(2)]
        for tg in range(2):
            norm_phase([(2 * tg, 0), (2 * tg + 1, 512)], hT, lambda g: hB[g % 2],
                       lambda g: (mv[l][:, 3, :], mv[l][:, 4, :]))
            ar.mark()
            uT = ar.alloc([64, 1024], BF16)
            uB = [P.buf("u%d" % i) for i in range(2)]
            sub = [(2 * tg, 0, 512), (2 * tg + 1, 512, 512)]

            rrot = Rot(ar, P, 2, [512], F32, "relu")

            def epi_u(mi, gid, ps, pB, m, n, rrot=rrot, uT=uT, uB=uB):
                s_ = gid % 2
                rt_, rB_ = rrot.next()
                ACT(rt_[:, 0:n], ps[:, 0:n], AF.Relu, [pB], [rB_])
                TTo("dve", uT[:, mi, s_ * 512:(s_ + 1) * 512], rt_[:, 0:n], rt_[:, 0:n], ALU.mult, [rB_], [uB[s_]])
            gemm_fm(w1_d[l], 0, 16, 0, DFF, hT, lambda g: hB[g % 2], sub, epi_u, WView(WROT, 16))
            pre, epi = make_resid(mv[l][:, 5, :])
            gemm_fm(w2_d[l], 0, 64, 0, D, uT, lambda g: uB[g % 2], sub, epi, WView(WROT, 64), bw=128, pre=pre)
            ar.release()
        ar.release()

    UB = Buf("U")
    fTB = Buf("fT")

    def phase_fourier():
        ar.mark()
        hT = ar.alloc([16, L], BF16)
        hB = [P.buf("fh%d" % g) for g in range(4)]
        norm_phase([(g, g * 512) for g in range(4)], hT, lambda g: hB[g], lambda g: (mv[1][:, 0, :], mv[1][:, 1, :]))
        dft = ar.alloc([2, 512], BF16)
        dB = P.buf("dft256")
        DMA("pool", dft, dft256_d.rearrange("(i p) n -> p i n", p=128), (), [dB])
        urot = Rot(ar, P, 3, [8, 512], BF16, "ust")
        for ti in range(16):
            ut, uB_ = urot.next()
            for g in range(8):
                ps, pB = next_ps()
                for i in range(2):
                    MM(ps, hT[:, 2 * g + i, ti * 128:(ti + 1) * 128], dft[:, i, :], i == 0, i == 1, [hB[ti // 4], dB], [pB])
                CP("act" if g % 2 else "dve", ut[:, g, :], ps, [pB], [uB_])
            DMA("sp", U_d[ti * 128:(ti + 1) * 128], ut, [uB_], [UB])
        ar.release()
        ar.mark()
        Uh = ar.alloc([16, 4, 512], BF16)
        UhB = P.buf("Uh")
        trot = Rot(ar, P, 2, [2, 16, 512], BF16, "dftL")
        tcol = ar.alloc([16, 16], BF16)
        tcB = P.buf("tcol")
        DMA("sp", tcol, dftL_d[0, :, 1024:1040].rearrange("(j p) n -> p j n", p=128), (), [tcB])
        o1rot = Rot(ar, P, 3, [512], BF16, "fo1")
        o2rot = Rot(ar, P, 3, [520], BF16, "fo2")
        qrot = Rot(ar, P, 2, [512], F32, "fq")
        for half in range(2):
            for ti in range(16):
                DMA("sp", Uh[:, ti, :, :], U_d[ti * 128:(ti + 1) * 128, half * 4:(half + 1) * 4, :], [UB], [UhB])
            for tb in range(2):
                tab, tB = trot.next()
                for cs in range(2):
                    DMA("sp", tab[:, cs, :, :], dftL_d[cs, :, tb * 512:(tb + 1) * 512].rearrange("(j p) n -> p j n", p=128), (), [tB])
                for gl in range(4):
                    for dh in range(2):
                        dch = (half * 4 + gl) * 2 + dh
                        psP, pPB = next_ps()
                        for ti in range(16):
                            MM(psP, Uh[:, ti, gl, dh * 128:(dh + 1) * 128], tab[:, 0, ti, :], ti == 0, ti == 15, [UhB, tB], [pPB])
                        psQ, pQB = next_ps()
                        for ti in range(16):
                            MM(psQ, Uh[:, ti, gl, 256 + dh * 128:256 + (dh + 1) * 128], tab[:, 1, ti, :], ti == 0, ti == 15,
                               [UhB, tB], [pQB])
                        q32, qB = qrot.next()
                        CP("act", q32, psQ, [pQB], [qB])
                        o1, o1B = o1rot.next()
                        TTo("dve", o1, psP, q32, ALU.subtract, [pPB, qB], [o1B])
                        DMA("sp", fT_d[dch * 128:(dch + 1) * 128, tb * 512:(tb + 1) * 512], o1, [o1B], [fTB])
                        o2, o2B = o2rot.next()
                        TTo("dve", o2[:, 1:513][:, ::-1], psP, q32, ALU.add, [pPB, qB], [o2B])
                        if tb == 0:
                            DMA("sp", fT_d[dch * 128:(dch + 1) * 128, 1537:2048], o2[:, 1:512], [o2B], [fTB])
                        else:
                            ps3, p3B = next_ps()
                            for ti in range(16):
                                MM(ps3[:, 0:1], Uh[:, ti, gl, dh * 128:(dh + 1) * 128], tcol[:, ti, 0:1], ti == 0, ti == 15,
                                   [UhB, tcB], [p3B])
                            CP("act", o2[:, 0:1], ps3[:, 0:1], [p3B], [o2B])
                            DMA("sp", fT_d[dch * 128:(dch + 1) * 128, 1024:1537], o2[:, 0:513], [o2B], [fTB])
        ar.release()

    def phase_out():
        ar.mark()
        rin = Rot(ar, P, 2, [16, 512], F32, "oin")
        rout = Rot(ar, P, 2, [4, D], F32, "oout")
        for g in range(4):
            xg, xB = rin.next()
            DMA("sp", xg, xT_v[:, :, g * 512:(g + 1) * 512], [XB[j][g] for j in range(16)], [xB])
            og, oB = rout.next()
            for i in range(4):
                for jb in range(4):
                    ps, pB = next_ps()
                    for jj in range(4):
                        j = jb * 4 + jj
                        TR(ps[:, jj * 128:(jj + 1) * 128], xg[:, j, i * 128:(i + 1) * 128], ident, [xB, identB], [pB])
                    CP("act" if jb % 2 else "dve", og[:, i, jb * 512:(jb + 1) * 512], ps, [pB], [oB])
            DMA("sp", out_d[g * 512:(g + 1) * 512, :].rearrange("(i p) d -> p i d", p=128), og, [oB], [Buf()])
        ar.release()

    phase_transpose_in()
    if stop_after == "tin":
        return finish()
    phase_inproj()
    if stop_after == "inproj":
        return finish()
    phase_attn()
    if stop_after == "attn":
        return finish()
    phase_gla()
    if stop_after == "gla":
        return finish()
    phase_outproj(wout_d[:, :], 0, False, catT_v, CTB)
    if stop_after == "outproj":
        return finish()
    phase_mlp(0)
    if stop_after == "mlp0":
        return finish()
    phase_fourier()
    if stop_after == "fourier":
        return finish()
    phase_outproj(cw_d[:, :], 1, True, fT_d.rearrange("(j p) t -> p j t", p=128), [fTB])
    if stop_after == "cw":
        return finish()
    phase_mlp(1)
    phase_out()
    return finish()


def _fm(v):
    return np.ascontiguousarray(np.asarray(v, np.float32).reshape(-1, 128).T)


def host_consts():
    c = {}
    c["ident"] = np.eye(128, dtype=np.float32)
    R = np.zeros((128, 128), np.float32)
    for base in (0, 64):
        for i in range(32):
            R[base + i, base + i + 32] = -1.0
            R[base + i + 32, base + i] = 1.0
    c["rt"] = np.ascontiguousarray(R.T)
    kk = np.arange(128)[:, None]
    qq = np.arange(128)[None, :]
    am = np.zeros((128, 2, 128), np.float32)
    am[:, 0, :] = (kk >= qq)
    am[:, 1, :] = (kk <= qq)
    c["amask"] = am
    s = np.arange(128)[:, None]
    cc_ = np.arange(128)[None, :]
    tri = np.zeros((128, 6, 128), np.float32)
    tri[:, 0, :] = (s <= cc_) * (-1.0 / 16)
    tri[:, 1, :] = (s >= cc_) * (-1.0 / 16)
    tri[:, 2, :] = (s > cc_) * (-1.0 / 16)
    tri[:, 3, :] = (s < cc_) * (-1.0 / 16)
    tri[:, 4, :] = (s <= cc_)
    tri[:, 5, :] = (s >= cc_)
    c["tri"] = tri
    t = np.arange(L)
    row = (t // 64).astype(np.float32)
    col = (t % 64).astype(np.float32)
    inv = (10000.0 ** (-np.arange(32, dtype=np.float32) / 32)).astype(np.float32)
    ang_r = row[:, None] * inv
    ang_c = col[:, None] * inv
    cosT = np.zeros((128, L), np.float32)
    sinT = np.zeros((128, L), np.float32)
    for i in range(128):
        a = ang_r if i < 64 else ang_c
        cosT[i] = np.cos(a[:, i % 32])
        sinT[i] = np.sin(a[:, i % 32])
    c["cosT"] = cosT
    c["sinT"] = sinT
    k = np.arange(256)
    a256 = 2 * np.pi * ((k[:, None] * k[None, :]) % 256) / 256
    c["dft256"] = np.concatenate([np.cos(a256), np.sin(a256)], axis=1).astype(np.float32) / 16.0
    tt = np.arange(L, dtype=np.int64)
    aL = 2 * np.pi * ((tt[:, None] * tt[None, :]) % L).astype(np.float64) / L
    sc = 1.0 / np.sqrt(L)
    c["dftL"] = np.ascontiguousarray(np.stack([np.cos(aL) * sc, np.sin(aL) * sc])[:, :, 0:1040]).astype(np.float32).astype(ml_dtypes.bfloat16)
    return c


def make_in_maps(inputs, cores):
    consts = host_consts()
    f = lambda k: np.asarray(inputs[k], np.float32)
    vec = np.zeros((128, NVEC), np.float32)
    vec[:, VC["NM0"]:VC["NM0"] + 16] = _fm(f("norm_mix")[0])
    vec[:, VC["NM1"]:VC["NM1"] + 16] = _fm(f("norm_mix")[1])
    vec[:, VC["NL0"]:VC["NL0"] + 16] = _fm(f("norm_mlp")[0])
    vec[:, VC["NL1"]:VC["NL1"] + 16] = _fm(f("norm_mlp")[1])
    vec[:, VC["CB"]:VC["CB"] + 16] = _fm(f("c_b_out")[0])
    vec[:, VC["AB0"]:VC["AB0"] + 96] = _fm(f("ada_b")[0])
    vec[:, VC["AB1"]:VC["AB1"] + 96] = _fm(f("ada_b")[1])
    vec[:, VC["QN"]] = f("ab_q_norm")[0]
    vec[:, VC["KN"]] = f("ab_k_norm")[0]
    vec[:, VC["GN"]:VC["GN"] + 2] = _fm(f("ab_gla_norm")[0])
    vec[:, VC["SK"]:VC["SK"] + 8] = np.tile(f("ab_sink")[0][None, :], (128, 1))
    gk = np.stack([np.concatenate([f("ab_gk_f")[0], f("ab_gk_f_bias")[0][None, :]], 0),
                   np.concatenate([f("ab_gk_b")[0], f("ab_gk_b_bias")[0][None, :]], 0)]).astype(np.float32)
    shared = dict(vec=vec, gk=gk, gnrow=f("ab_gla_norm")[0][None, :].copy(), ada_w=f("ada_w"), w_in=f("ab_w_in")[0],
                  w_out=f("ab_w_out")[0], c_w=f("c_w_out")[0], w1=f("mlp_w1"), w2=f("mlp_w2"), **consts)
    maps = []
    cfm = _fm(f("c_ctx"))
    for b in cores:
        m = dict(shared)
        m["x"] = np.ascontiguousarray(f("x")[b])
        m["ctx"] = np.ascontiguousarray(f("ctx")[b])
        m["cc"] = np.ascontiguousarray(np.concatenate([_fm(f("c")[b]), cfm], axis=1))
        maps.append(m)
    return maps


def kernel(**inputs):
    nc = build_program()
    maps = make_in_maps(inputs, list(range(8)))
    res = run_bass_kernel_spmd(nc, maps, core_ids=list(range(8)))
    return np.stack([np.asarray(r["out"], np.float32) for r in res.results], axis=0)
```
